# Optimizing a Trainium2 kernel written in Bass

```python
import math
import jax
import jax.numpy as jnp
from jax import lax
import numpy as np

D_MODEL = 1024
BATCH = 8
SEQ = 2048
DEPTH = 2

CTX_LEN = 256
GRID_W = 64
EPS = 1e-6
N_MOD = 6
HEAD_DIM = 64
ATTN_WIDTH = D_MODEL // 2
N_HEADS = ATTN_WIDTH // HEAD_DIM
N_KV_HEADS = 2
Q_PER_KV = N_HEADS // N_KV_HEADS
ROPE_THETA = 10000.0
ATTN_BLOCK = 128
SSD_WIDTH = D_MODEL - ATTN_WIDTH
SSD_HEAD_DIM = 64
SSD_HEADS = SSD_WIDTH // SSD_HEAD_DIM
SSD_GROUPS = 2
HEADS_PER_GROUP = SSD_HEADS // SSD_GROUPS
D_STATE = 128
CONV_WIDTH = 5
SSD_CHUNK = 128
D_FF = -(-8 * D_MODEL // (3 * 256)) * 256
Q_DIM = N_HEADS * HEAD_DIM
KV_DIM = N_KV_HEADS * HEAD_DIM
BC_DIM = SSD_GROUPS * D_STATE
XBC_DIM = SSD_WIDTH + 2 * BC_DIM
IN_DIM = Q_DIM + 2 * KV_DIM + SSD_WIDTH + XBC_DIM + SSD_HEADS
IN_SPLITS = (Q_DIM, Q_DIM + KV_DIM, Q_DIM + 2 * KV_DIM, Q_DIM + 2 * KV_DIM + SSD_WIDTH,
             Q_DIM + 2 * KV_DIM + SSD_WIDTH + XBC_DIM)

kernel_name = 'hybrid_attn_ssd_dit_trunk'


def rmsnorm(x, w):
    xf = x.astype(jnp.float32)
    y = xf * lax.rsqrt(jnp.mean(xf * xf, axis=-1, keepdims=True) + EPS)
    return (y * w.astype(jnp.float32)).astype(x.dtype)


def axial_rope_tables(rows):
    row = jnp.repeat(jnp.arange(rows, dtype=jnp.float32), GRID_W)
    col = jnp.tile(jnp.arange(GRID_W, dtype=jnp.float32), rows)
    n_freq = HEAD_DIM // 4
    freqs = ROPE_THETA ** (-jnp.arange(n_freq, dtype=jnp.float32) / n_freq)
    ang = jnp.concatenate([row[:, None] * freqs, col[:, None] * freqs], axis=-1)
    return jnp.cos(ang), jnp.sin(ang)


def apply_rope(x, cos, sin):
    half = HEAD_DIM // 2
    x1, x2 = x[..., :half], x[..., half:]
    c = cos[None, :, None, :].astype(x.dtype)
    s = sin[None, :, None, :].astype(x.dtype)
    return jnp.concatenate([x1 * c - x2 * s, x1 * s + x2 * c], axis=-1)


def block_attention(q, k, v):
    b, n = q.shape[:2]
    nb = n // ATTN_BLOCK
    qb = q.reshape(b, nb, ATTN_BLOCK, N_KV_HEADS, Q_PER_KV, HEAD_DIM).transpose(1, 0, 2, 3, 4, 5)
    scale = HEAD_DIM ** -0.5

    def one_block(q_blk):
        s = jnp.einsum('bqkgd,bskd->bkgqs', q_blk, k).astype(jnp.float32) * scale
        p = jax.nn.softmax(s, axis=-1).astype(v.dtype)
        return jnp.einsum('bkgqs,bskd->bqkgd', p, v)

    o = lax.map(one_block, qb)
    return o.transpose(1, 0, 2, 3, 4, 5).reshape(b, n, N_HEADS * HEAD_DIM)


def depthwise_conv_silu(x, w, bias):
    y = lax.conv_general_dilated(
        x, w[:, None, :].astype(x.dtype), window_strides=(1,),
        padding=[(CONV_WIDTH // 2, CONV_WIDTH // 2)],
        dimension_numbers=('NWC', 'WIO', 'NWC'), feature_group_count=x.shape[-1])
    return jax.nn.silu(y + bias.astype(x.dtype))


def ssd_scan(x, dt, A, B, C, h0):
    b, L, H, P = x.shape
    nc = L // SSD_CHUNK
    f32 = jnp.float32
    Bh = jnp.repeat(B.astype(f32), HEADS_PER_GROUP, axis=2).reshape(b, nc, SSD_CHUNK, H, D_STATE)
    Ch = jnp.repeat(C.astype(f32), HEADS_PER_GROUP, axis=2).reshape(b, nc, SSD_CHUNK, H, D_STATE)
    xdt = (x.astype(f32) * dt[..., None]).reshape(b, nc, SSD_CHUNK, H, P)
    acum = jnp.cumsum((dt * A).reshape(b, nc, SSD_CHUNK, H), axis=2)
    lower = jnp.tril(jnp.ones((SSD_CHUNK, SSD_CHUNK), dtype=bool))[None, None, :, :, None]
    seg = acum[:, :, :, None, :] - acum[:, :, None, :, :]
    decay = jnp.exp(jnp.where(lower, seg, -jnp.inf))
    scores = jnp.einsum('bcihn,bcjhn->bcijh', Ch, Bh) * decay
    y_intra = jnp.einsum('bcijh,bcjhp->bcihp', scores, xdt)
    decay_to_end = jnp.exp(acum[:, :, -1:, :] - acum)
    chunk_states = jnp.einsum('bcjhn,bcjh,bcjhp->bchpn', Bh, decay_to_end, xdt)
    chunk_decay = jnp.exp(acum[:, :, -1, :])

    def step(h, inp):
        s_c, d_c = inp
        return h * d_c[:, :, None, None] + s_c, h

    h_final, h_in = lax.scan(step, h0, (chunk_states.transpose(1, 0, 2, 3, 4), chunk_decay.transpose(1, 0, 2)))
    h_in = h_in.transpose(1, 0, 2, 3, 4)
    y_inter = jnp.einsum('bcihn,bcih,bchpn->bcihp', Ch, jnp.exp(acum), h_in)
    return (y_intra + y_inter).reshape(b, L, H, P), h_final


def hybrid_mixer(h_lat, h_ctx, cos, sin, w_in, q_norm, k_norm, conv_w, conv_b, dt_bias_fwd, dt_bias_bwd,
                 a_log_fwd, a_log_bwd, d_skip, ssd_norm, w_out, with_ctx_out):
    b, n_lat, _ = h_lat.shape
    n_ctx = h_ctx.shape[1]
    f32 = jnp.float32
    q_l, k_l, v_l, z_l, xbc_l, dt_l = jnp.split(h_lat @ w_in, IN_SPLITS, axis=-1)
    q_c, k_c, v_c, z_c, xbc_c, dt_c = jnp.split(h_ctx @ w_in, IN_SPLITS, axis=-1)

    def heads(q, k, v, n):
        q = rmsnorm(q.reshape(b, n, N_HEADS, HEAD_DIM), q_norm)
        k = rmsnorm(k.reshape(b, n, N_KV_HEADS, HEAD_DIM), k_norm)
        return q, k, v.reshape(b, n, N_KV_HEADS, HEAD_DIM)

    q_l, k_l, v_l = heads(q_l, k_l, v_l, n_lat)
    q_c, k_c, v_c = heads(q_c, k_c, v_c, n_ctx)
    q_l = apply_rope(q_l, cos, sin)
    k_l = apply_rope(k_l, cos, sin)
    k_all = jnp.concatenate([k_c, k_l], axis=1)
    v_all = jnp.concatenate([v_c, v_l], axis=1)
    attn_l = block_attention(q_l, k_all, v_all)

    def ssd_inputs(xbc, n):
        xbc = depthwise_conv_silu(xbc, conv_w, conv_b)
        xs, Bs, Cs = jnp.split(xbc, (SSD_WIDTH, SSD_WIDTH + BC_DIM), axis=-1)
        return (xs.reshape(b, n, SSD_HEADS, SSD_HEAD_DIM), Bs.reshape(b, n, SSD_GROUPS, D_STATE),
                Cs.reshape(b, n, SSD_GROUPS, D_STATE))

    xs_l, B_l, C_l = ssd_inputs(xbc_l, n_lat)
    xs_c, B_c, C_c = ssd_inputs(xbc_c, n_ctx)
    ys_l, ys_c = [], []
    for dt_bias, a_log, reverse in ((dt_bias_fwd, a_log_fwd, False), (dt_bias_bwd, a_log_bwd, True)):
        A = -jnp.exp(a_log.astype(f32))
        d_l = jax.nn.softplus(dt_l.astype(f32) + dt_bias.astype(f32))
        d_c = jax.nn.softplus(dt_c.astype(f32) + dt_bias.astype(f32))
        flip = (lambda t: jnp.flip(t, axis=1)) if reverse else (lambda t: t)
        h0 = jnp.zeros((b, SSD_HEADS, SSD_HEAD_DIM, D_STATE), f32)
        y_c, h_ctx_final = ssd_scan(flip(xs_c), flip(d_c), A, flip(B_c), flip(C_c), h0)
        y_l, _ = ssd_scan(flip(xs_l), flip(d_l), A, flip(B_l), flip(C_l), h_ctx_final)
        ys_l.append(flip(y_l))
        ys_c.append(flip(y_c))

    def ssd_output(y, xs, z, n):
        y = (y + d_skip.astype(f32)[:, None] * xs.astype(f32)).astype(z.dtype).reshape(b, n, SSD_WIDTH)
        return rmsnorm(y * jax.nn.silu(z), ssd_norm)

    out_l = jnp.concatenate([attn_l, ssd_output(ys_l[0] + ys_l[1], xs_l, z_l, n_lat)], axis=-1) @ w_out
    if not with_ctx_out:
        return out_l, None
    attn_c = block_attention(q_c, k_c, v_c)
    out_c = jnp.concatenate([attn_c, ssd_output(ys_c[0] + ys_c[1], xs_c, z_c, n_ctx)], axis=-1) @ w_out
    return out_l, out_c


def swiglu(h, w_gate, w_up, w_down):
    return (jax.nn.silu(h @ w_gate) * (h @ w_up)) @ w_down


def setup_inputs(seed: int = 0) -> dict:
    key = jax.random.key(seed)
    ks = jax.random.split(key, 24)
    f32 = jnp.float32
    L = DEPTH

    def nrm(k, shape, scale):
        return jax.random.normal(k, shape, f32) * scale

    def gain(k, shape):
        return 1.0 + 0.05 * jax.random.normal(k, shape, f32)

    dt0 = jnp.exp(jax.random.uniform(ks[14], (L, 2, SSD_HEADS), f32, math.log(1e-3), math.log(1e-1)))
    dt_bias = dt0 + jnp.log(-jnp.expm1(-dt0))
    a_log = jnp.log(jax.random.uniform(ks[15], (L, 2, SSD_HEADS), f32, 1.0, 16.0))
    nk = jax.random.split(ks[6], 4)
    return {
        'x': nrm(ks[0], (BATCH, SEQ, D_MODEL), 1.0),
        'c': nrm(ks[1], (BATCH, D_MODEL), 1.0),
        'ctx': nrm(ks[2], (BATCH, CTX_LEN, D_MODEL), 1.0),
        'c_ctx': nrm(ks[3], (D_MODEL,), 1.0),
        'w_mod': nrm(ks[4], (L, D_MODEL, N_MOD * D_MODEL), 0.5 * D_MODEL ** -0.5),
        'b_mod': nrm(ks[5], (L, N_MOD * D_MODEL), 0.02),
        'norm_mix_pre': gain(nk[0], (L, D_MODEL)),
        'norm_mix_post': gain(nk[1], (L, D_MODEL)),
        'norm_ffn_pre': gain(nk[2], (L, D_MODEL)),
        'norm_ffn_post': gain(nk[3], (L, D_MODEL)),
        'w_in': nrm(ks[10], (L, D_MODEL, IN_DIM), D_MODEL ** -0.5),
        'q_norm': gain(ks[11], (L, HEAD_DIM)),
        'k_norm': gain(ks[12], (L, HEAD_DIM)),
        'conv_w': nrm(ks[13], (L, CONV_WIDTH, XBC_DIM), CONV_WIDTH ** -0.5),
        'conv_b': nrm(ks[16], (L, XBC_DIM), 0.02),
        'dt_bias_fwd': dt_bias[:, 0],
        'dt_bias_bwd': dt_bias[:, 1],
        'a_log_fwd': a_log[:, 0],
        'a_log_bwd': a_log[:, 1],
        'd_skip': 1.0 + 0.1 * jax.random.normal(ks[17], (L, SSD_HEADS), f32),
        'ssd_norm': gain(ks[18], (L, SSD_WIDTH)),
        'w_out': nrm(ks[19], (L, D_MODEL, D_MODEL), D_MODEL ** -0.5),
        'w_gate': nrm(ks[20], (L, D_MODEL, D_FF), D_MODEL ** -0.5),
        'w_up': nrm(ks[21], (L, D_MODEL, D_FF), D_MODEL ** -0.5),
        'w_down': nrm(ks[22], (L, D_FF, D_MODEL), D_FF ** -0.5),
    }


def reference(x, c, ctx, c_ctx, w_mod, b_mod, norm_mix_pre, norm_mix_post, norm_ffn_pre, norm_ffn_post,
              w_in, q_norm, k_norm, conv_w, conv_b, dt_bias_fwd, dt_bias_bwd, a_log_fwd, a_log_bwd, d_skip,
              ssd_norm, w_out, w_gate, w_up, w_down):
    n_lat = x.shape[1]
    rows = n_lat // GRID_W
    cos, sin = axial_rope_tables(rows)
    silu_c = jax.nn.silu(c)
    silu_cc = jax.nn.silu(c_ctx)
    x_lat, x_ctx = x, ctx
    for l in range(DEPTH):
        update_ctx = l < DEPTH - 1
        mod_l = (silu_c @ w_mod[l] + b_mod[l])[:, None, :]
        mod_c = (silu_cc @ w_mod[l] + b_mod[l])[None, None, :]
        sh1_l, sc1_l, g1_l, sh2_l, sc2_l, g2_l = jnp.split(mod_l, N_MOD, axis=-1)
        sh1_c, sc1_c, g1_c, sh2_c, sc2_c, g2_c = jnp.split(mod_c, N_MOD, axis=-1)

        h_l = rmsnorm(x_lat, norm_mix_pre[l]) * (1.0 + sc1_l) + sh1_l
        h_c = rmsnorm(x_ctx, norm_mix_pre[l]) * (1.0 + sc1_c) + sh1_c
        m_l, m_c = hybrid_mixer(h_l, h_c, cos, sin, w_in[l], q_norm[l], k_norm[l], conv_w[l], conv_b[l],
                                dt_bias_fwd[l], dt_bias_bwd[l], a_log_fwd[l], a_log_bwd[l], d_skip[l],
                                ssd_norm[l], w_out[l], update_ctx)
        x_lat = x_lat + g1_l * rmsnorm(m_l, norm_mix_post[l])

        f_l = rmsnorm(x_lat, norm_ffn_pre[l]) * (1.0 + sc2_l) + sh2_l
        x_lat = x_lat + g2_l * rmsnorm(swiglu(f_l, w_gate[l], w_up[l], w_down[l]), norm_ffn_post[l])

        if update_ctx:
            x_ctx = x_ctx + g1_c * rmsnorm(m_c, norm_mix_post[l])
            f_c = rmsnorm(x_ctx, norm_ffn_pre[l]) * (1.0 + sc2_c) + sh2_c
            x_ctx = x_ctx + g2_c * rmsnorm(swiglu(f_c, w_gate[l], w_up[l], w_down[l]), norm_ffn_post[l])
    return x_lat
```

```python
import contextlib
import numpy as np
import concourse.bass as bass
import concourse.mybir as mybir
from concourse.bass_utils import run_bass_kernel_spmd

F32 = mybir.dt.float32
BF16 = mybir.dt.bfloat16
AF = mybir.ActivationFunctionType
ALU = mybir.AluOpType
AX = mybir.AxisListType

D = 1024
NLAT = 2048
NCTX = 256
NTOK = NLAT + NCTX
NT = NTOK // 128
DEPTH = 2
DFF = 2816
NF = DFF // 128
IN_DIM = 2312
EPS = 1e-6
NEG = -30000.0
FUSE_WAIT = True


class Buf:
    __slots__ = ("name", "w", "r", "dsem", "dcount", "_last_dma")

    def __init__(self, name):
        self.name = name
        self.w = None
        self.r = {}
        self.dsem = None
        self.dcount = 0
        self._last_dma = None


class Op:
    __slots__ = ("eng", "fn", "reads", "writes", "dma", "deps", "need_inc", "ticket",
                 "dbuf", "dval", "n_dma", "barrier")

    def __init__(self, eng, fn, reads, writes, dma=False, dbuf=None, n_dma=1):
        self.eng = eng
        self.fn = fn
        self.reads = reads
        self.writes = writes
        self.dma = dma
        self.deps = []
        self.need_inc = False
        self.ticket = None
        self.dbuf = dbuf
        self.dval = None
        self.n_dma = n_dma
        self.barrier = False


class Rec:
    def __init__(self):
        self.calls = []

    def __getattr__(self, name):
        def f(*a, **kw):
            self.calls.append((name, a, kw))
            return None
        return f


def _bind(fn):
    r = Rec()
    fn(r)
    calls = r.calls

    def replay(E):
        return [getattr(E, name)(*a, **kw) for (name, a, kw) in calls]
    return replay


class Prog:
    ENGS = ("pe", "act", "dve", "pool", "sp")

    def __init__(self):
        self.nc = bass.Bass("TRN2", target_bir_lowering=False)
        self.ops = []
        self.bufs = {}

    def eng(self, name):
        nc = self.nc
        return {"pe": nc.tensor, "act": nc.scalar, "dve": nc.vector, "pool": nc.gpsimd,
                "sp": nc.sync}[name]

    def buf(self, name):
        b = self.bufs.get(name)
        if b is None:
            b = Buf(name)
            self.bufs[name] = b
        return b

    mute = False
    region = False
    limit = 10 ** 9
    count = 0

    def _skip(self):
        if self.mute:
            return True
        if self.region:
            self.count += 1
            return self.count > self.limit
        return False

    def op(self, eng, fn, reads=(), writes=()):
        if self._skip():
            return None
        pr = [b for b in reads if isinstance(b, str) and b[:2] in ("ps", "pb")]
        if pr:
            reads = [b for b in reads if b not in pr]
            writes = list(writes) + [b for b in pr if b not in writes]
        o = Op(eng, _bind(fn), [self.buf(b) if isinstance(b, str) else b for b in reads],
               [self.buf(b) if isinstance(b, str) else b for b in writes])
        self.ops.append(o)
        return o

    def dma(self, eng, fn, reads=(), writes=(), n_dma=1, dbuf=None):
        if self._skip():
            return None
        reads = [self.buf(b) if isinstance(b, str) else b for b in reads]
        writes = [self.buf(b) if isinstance(b, str) else b for b in writes]
        dbuf = self.buf(dbuf) if dbuf is not None else (writes[0] if writes else reads[0])
        o = Op(eng, _bind(fn), reads, writes, dma=True, dbuf=dbuf, n_dma=n_dma)
        self.ops.append(o)
        return o

    def barrier(self):
        o = Op("sp", None, [], [])
        o.barrier = True
        self.ops.append(o)

    def finalize(self):
        nc = self.nc
        last = {e: None for e in self.ENGS}
        dma_bufs_seen = []
        for o in self.ops:
            if o.barrier:
                o.deps = [x for x in last.values() if x is not None]
                for d in o.deps:
                    d.need_inc = True
                o.dval = list(dma_bufs_seen)
                for b in self.bufs.values():
                    b.w = None
                    b.r = {}
                    b._last_dma = None
                continue
            deps = []
            for b in o.reads:
                if b.w is not None:
                    deps.append(b.w)
            for b in o.writes:
                if b.w is not None:
                    deps.append(b.w)
                deps.extend(b.r.values())
            if o.dma:
                prev = o.dbuf._last_dma
                if prev is not None:
                    deps.append(prev)
                o.dbuf._last_dma = o
                if o.dbuf not in dma_bufs_seen:
                    dma_bufs_seen.append(o.dbuf)
            seen = set()
            for d in deps:
                if d is o or id(d) in seen:
                    continue
                seen.add(id(d))
                if (not d.dma) and (not o.dma) and d.eng == o.eng and d.eng == "pe":
                    continue
                o.deps.append(d)
                d.need_inc = True
            rk = ("d", id(o.dbuf)) if o.dma else o.eng
            for b in o.reads:
                b.r[rk] = o
            for b in o.writes:
                b.w = o
                b.r = {}
            if not o.dma:
                last[o.eng] = o
        counts = {e: 0 for e in self.ENGS}
        dbufs = []
        for o in self.ops:
            if o.barrier:
                o.ticket = [(b, b.dcount) for b in o.dval]
                continue
            if o.dma:
                b = o.dbuf
                if b.dsem is None:
                    b.dsem = len(dbufs)
                    dbufs.append(b)
                b.dcount += 16 * o.n_dma
                o.dval = b.dcount
            elif o.need_inc:
                counts[o.eng] += 1
                o.ticket = counts[o.eng]
        self.n_dsem = len(dbufs)
        with contextlib.ExitStack() as st:
            esem = {e: st.enter_context(nc.semaphore("s_" + e)) for e in self.ENGS}
            dsem = [st.enter_context(nc.semaphore("d%d" % i)) for i in range(len(dbufs))]
            waited = {e: {} for e in self.ENGS}

            def do_wait(ename, key, val):
                w = waited[ename]
                if w.get(key, 0) >= val:
                    return
                w[key] = val
                sem = dsem[key[1]] if key[0] == "d" else esem[key[1]]
                self.eng(ename).wait_ge(sem, val)

            for o in self.ops:
                if o.barrier:
                    for e in self.ENGS:
                        for d in o.deps:
                            if d.eng != e:
                                do_wait(e, ("e", d.eng), d.ticket)
                        for b, cnt in o.ticket:
                            do_wait(e, ("d", b.dsem), cnt)
                    continue
                need = {}
                for d in o.deps:
                    if d.dma:
                        key = ("d", d.dbuf.dsem)
                        val = d.dval
                    else:
                        key = ("e", d.eng)
                        val = d.ticket
                    if need.get(key, 0) < val:
                        need[key] = val
                pend = [(key, val) for key, val in need.items() if waited[o.eng].get(key, 0) < val]
                fused = pend.pop() if (pend and FUSE_WAIT and o.eng != "pe") else None
                for key, val in pend:
                    do_wait(o.eng, key, val)
                E = self.eng(o.eng)
                if o.dma:
                    insts = o.fn(E)
                    assert len(insts) == o.n_dma, (len(insts), o.n_dma)
                    for ins in insts:
                        ins.then_inc(dsem[o.dbuf.dsem], 16)
                else:
                    insts = o.fn(E)
                    assert len(insts) == 1
                    if o.need_inc:
                        insts[0].then_inc(esem[o.eng], 1)
                if fused is not None:
                    key, val = fused
                    waited[o.eng][key] = val
                    sem = dsem[key[1]] if key[0] == "d" else esem[key[1]]
                    insts[0]._wait_ge(sem, val)
            for b in dbufs:
                do_wait("sp", ("d", b.dsem), b.dcount)
        return nc


class Arena:
    def __init__(self, nc, base, top):
        self.nc = nc
        self.base = (base + 31) // 32 * 32
        self.top = top
        self.ptr = self.base
        self.n = 0

    def alloc(self, name, shape, dtype):
        nbytes = int(np.prod(shape[1:])) * (2 if dtype == BF16 else 4)
        off = self.ptr
        self.ptr = (off + nbytes + 31) // 32 * 32
        assert self.ptr <= self.top, ("SBUF overflow", name, self.ptr, self.top)
        self.n += 1
        return self.nc.alloc_sbuf_tensor_at("%s_%d" % (name, self.n), list(shape), dtype, offset=off)

    def mark(self):
        return self.ptr

    def release(self, m):
        self.ptr = m


def host_consts():
    t = np.arange(128)
    U = (t[:, None] <= t[None, :]).astype(np.float32)
    UT = (t[:, None] >= t[None, :]).astype(np.float32)
    mf = np.where(t[:, None] <= t[None, :], 0.0, NEG).astype(np.float32)
    mb = np.where(t[:, None] >= t[None, :], 0.0, NEG).astype(np.float32)
    maskf = np.tile(mf[:, None, :], (1, 8, 1)).reshape(128, 1024)
    maskb = np.tile(mb[:, None, :], (1, 8, 1)).reshape(128, 1024)
    ident = np.eye(128, dtype=np.float32)
    ones = np.ones((128, 128), np.float32)
    n_freq = 16
    freqs = (10000.0 ** (-np.arange(n_freq, dtype=np.float32) / n_freq)).astype(np.float32)
    row = np.repeat(np.arange(NLAT // 64, dtype=np.float32), 64)
    col = np.tile(np.arange(64, dtype=np.float32), NLAT // 64)
    ang = np.concatenate([row[:, None] * freqs, col[:, None] * freqs], axis=-1).astype(np.float32)
    cos = np.ones((NTOK, 32), np.float32)
    sin = np.zeros((NTOK, 32), np.float32)
    cos[:NLAT] = np.cos(ang)
    sin[:NLAT] = np.sin(ang)
    cmat = np.concatenate([U, UT, ident, ones], axis=1)
    return {"c_f32": cmat, "c_maskf": maskf, "c_maskb": maskb,
            "c_cos": cos.reshape(NT, 128, 32).transpose(1, 0, 2).copy(),
            "c_sin": sin.reshape(NT, 128, 32).transpose(1, 0, 2).copy()}


def build(depth=DEPTH, dbg=None):
    P = Prog()
    nc = P.nc
    dt_in = lambda name, shape: nc.dram_tensor(name, list(shape), F32, kind="ExternalInput").ap()
    x_d = dt_in("x", [NLAT, D])
    ctx_d = dt_in("ctx", [NCTX, D])
    c_d = dt_in("c", [D])
    cc_d = dt_in("c_ctx", [D])
    w_mod = dt_in("w_mod", [DEPTH, D, 6 * D])
    b_mod = dt_in("b_mod", [DEPTH, 6 * D])
    n_mix_pre = dt_in("norm_mix_pre", [DEPTH, D])
    n_mix_post = dt_in("norm_mix_post", [DEPTH, D])
    n_ffn_pre = dt_in("norm_ffn_pre", [DEPTH, D])
    n_ffn_post = dt_in("norm_ffn_post", [DEPTH, D])
    w_in = dt_in("w_in", [DEPTH, D, IN_DIM])
    q_norm = dt_in("q_norm", [DEPTH, 64])
    k_norm = dt_in("k_norm", [DEPTH, 64])
    conv_w = dt_in("conv_w", [DEPTH, 5, D])
    conv_b = dt_in("conv_b", [DEPTH, D])
    dtb_f = dt_in("dt_bias_fwd", [DEPTH, 8])
    dtb_b = dt_in("dt_bias_bwd", [DEPTH, 8])
    alog_f = dt_in("a_log_fwd", [DEPTH, 8])
    alog_b = dt_in("a_log_bwd", [DEPTH, 8])
    d_skip = dt_in("d_skip", [DEPTH, 8])
    ssd_norm = dt_in("ssd_norm", [DEPTH, 512])
    w_out = dt_in("w_out", [DEPTH, D, D])
    w_gate = dt_in("w_gate", [DEPTH, D, DFF])
    w_up = dt_in("w_up", [DEPTH, D, DFF])
    w_down = dt_in("w_down", [DEPTH, DFF, D])
    c_f32 = dt_in("c_f32", [128, 512])
    c_maskf = dt_in("c_maskf", [128, 1024])
    c_maskb = dt_in("c_maskb", [128, 1024])
    c_cos = dt_in("c_cos", [128, NT, 32])
    c_sin = dt_in("c_sin", [128, NT, 32])
    out_d = nc.dram_tensor("out", [NLAT, D], F32, kind="ExternalOutput").ap()
    modrow_d = nc.dram_tensor("modrow_s", [DEPTH, 2, 6 * D], F32).ap()
    raw_d = nc.dram_tensor("raw_s", [D, NTOK], BF16).ap()
    sz_d = nc.dram_tensor("sz_s", [NTOK, 512], BF16).ap()
    g_d = nc.dram_tensor("g_s", [NTOK, 512], BF16).ap()
    wsc_d = nc.dram_tensor("wsc_s", [NF, 128, 2048], BF16).ap()
    dbg_d = {}
    if dbg:
        for name, shape in dbg.items():
            if name.startswith("_"):
                continue
            dbg_d[name] = nc.dram_tensor("dbg_" + name, list(shape), F32, kind="ExternalOutput").ap()

    A = Arena(nc, nc.sbuf_base, nc.sbuf_top)
    op, dma = P.op, P.dma
    with contextlib.ExitStack() as st:
        PS = st.enter_context(nc.psum_tensor("ps", [128, 8 * 512], F32))

        def bank(b, n=1):
            return PS[:, b * 512:(b + n) * 512]

        X = A.alloc("X", [128, NT, D], F32)
        CF = A.alloc("CF", [128, 512], F32)
        Uf, UTf, IDf, ONf = CF[:, 0:128], CF[:, 128:256], CF[:, 256:384], CF[:, 384:512]
        IDb = A.alloc("IDb", [128, 128], BF16)
        MKf = A.alloc("MKf", [128, 1024], BF16)
        MKb = A.alloc("MKb", [128, 1024], BF16)
        modT = A.alloc("modT", [128, DEPTH, 48, 2], F32)
        nrmT = A.alloc("nrmT", [128, DEPTH, 2, 8], F32)
        AT = A.alloc("AT", [128, 2, 8], F32)
        ST = A.alloc("ST", [128, 2, 8], F32)
        ssq = A.alloc("ssq", [128, 8], F32)
        small_mark = A.mark()

        dma("sp", lambda e: [e.dma_start(out=CF[:], in_=c_f32)], writes=["CF"])
        dma("pool", lambda e: [e.dma_start(out=MKf[:], in_=c_maskf)], writes=["MKf"])
        dma("pool", lambda e: [e.dma_start(out=MKb[:], in_=c_maskb)], writes=["MKb"])
        op("dve", lambda e: e.tensor_copy(IDb[:], IDf), reads=["CF"], writes=["IDb"])
        for t in range(NT):
            src = x_d[t * 128:(t + 1) * 128, :] if t < 16 else ctx_d[(t - 16) * 128:(t - 15) * 128, :]
            dma("sp", lambda e, t=t, src=src: [e.dma_start(out=X[:, t, :], in_=src)], writes=["X%d" % t])
        for l in range(DEPTH):
            for i, nw in enumerate((n_mix_pre, n_ffn_pre)):
                dma("sp", lambda e, l=l, i=i, nw=nw: [e.dma_start(
                    out=nrmT[:, l, i, :], in_=nw[l].rearrange("(kc p) -> p kc", p=128),
                    allow_slow_non_contiguous=True)], writes=["nrmT"])

        m0 = A.mark()
        stop = (dbg or {}).get("_stop")
        scv = A.alloc("scv", [128, 8, 2], F32)
        wm = [A.alloc("wm%d" % i, [128, 8, 512], F32) for i in range(3)]
        bmt = [A.alloc("bmt%d" % i, [2, 512], F32) for i in range(3)]
        mrow = [A.alloc("mrow%d" % i, [2, 512], F32) for i in range(3)]
        dma("sp", lambda e: [e.dma_start(out=scv[:, :, 0], in_=c_d.rearrange("(kc p) -> p kc", p=128),
                                         allow_slow_non_contiguous=True)], writes=["scv"])
        dma("sp", lambda e: [e.dma_start(out=scv[:, :, 1], in_=cc_d.rearrange("(kc p) -> p kc", p=128),
                                         allow_slow_non_contiguous=True)], writes=["scv"])
        op("act", lambda e: e.activation(scv[:], scv[:], AF.Silu), reads=["scv"], writes=["scv"])
        slabs = [(l, n) for l in range(depth if stop != "init" else 0)
                 for n in range({"s0a": 1, "s0b": 2, "s0c": 3, "s0d": 6, "s0e": 9}.get(stop, 12))]

        def s0_load(i):
            l, n = slabs[i]
            s = i % 3
            dma("sp", lambda e: [e.dma_start(
                out=wm[s][:], in_=w_mod[l][:, n * 512:(n + 1) * 512].rearrange("(kc p) n -> p kc n", p=128))],
                writes=["wm%d" % s])
            dma("sp", lambda e: [e.dma_start(
                out=bmt[s][:], in_=b_mod[l:l + 1, n * 512:(n + 1) * 512].partition_broadcast(2))],
                writes=["bmt%d" % s])

        for i in range(min(2, len(slabs))):
            s0_load(i)
        def s0_mm(i):
            l, n = slabs[i]
            s = i % 3
            bm = 0 if i % 2 == 0 else 2
            for kc in range(8):
                op("pe", lambda e: e.matmul(bank(bm)[0:2, :], scv[:, kc, :], wm[s][:, kc, :],
                                            start=(kc == 0), stop=(kc == 7)),
                   reads=["scv", "wm%d" % s], writes=["psmod%d" % bm])

        def s0_post(i):
            l, n = slabs[i]
            s = i % 3
            bm, bt = (0, 1) if i % 2 == 0 else (2, 3)
            op("dve", lambda e: e.tensor_tensor(mrow[s][:], bank(bm)[0:2, :], bmt[s][:], ALU.add),
               reads=["psmod%d" % bm, "bmt%d" % s], writes=["mrow%d" % s])
            for j in range(4):
                op("pe", lambda e: e.transpose(bank(bt)[:, j * 2:(j + 1) * 2],
                                               mrow[s][0:2, j * 128:(j + 1) * 128], IDf[0:2, 0:2]),
                   reads=["mrow%d" % s, "CF"], writes=["psmt%d" % bt])
            op("act", lambda e: e.copy(
                modT[:, l, n * 4:(n + 1) * 4, :], bank(bt)[:, 0:8].rearrange("p (j t) -> p j t", t=2)),
               reads=["psmt%d" % bt], writes=["modT"])
            dma("act", lambda e: [e.dma_start(out=modrow_d[l, :, n * 512:(n + 1) * 512], in_=mrow[s][:])],
                reads=["mrow%d" % s])

        if slabs:
            s0_mm(0)
        for i in range(len(slabs)):
            if i + 2 < len(slabs):
                s0_load(i + 2)
            if i + 1 < len(slabs):
                s0_mm(i + 1)
            s0_post(i)
        P.barrier()
        A.release(m0)
        stop = (dbg or {}).get("_stop")

        def prep_at(l, i):
            sh0, sc0 = (3 * i) * 8, (3 * i + 1) * 8
            for typ in range(2):
                op("dve", lambda e, typ=typ: e.scalar_tensor_tensor(
                    AT[:, typ, :], modT[:, l, sc0:sc0 + 8, typ], 1.0, nrmT[:, l, i, :], ALU.add, ALU.mult),
                   reads=["modT", "nrmT"], writes=["AT"])
                op("dve", lambda e, typ=typ: e.tensor_copy(ST[:, typ, :], modT[:, l, sh0:sh0 + 8, typ]),
                   reads=["modT"], writes=["ST"])

        def prep_gb(l, i, GB, tmpw):
            g0 = 3 * i + 2
            npost = n_mix_post if i == 0 else n_ffn_post
            dma("sp", lambda e: [e.dma_start(out=tmpw[:], in_=npost[l:l + 1, :].partition_broadcast(128))],
                writes=["tmpw"])
            for typ in range(2):
                dma("sp", lambda e, typ=typ: [e.dma_start(
                    out=GB[:, typ, :], in_=modrow_d[l, typ:typ + 1, g0 * D:(g0 + 1) * D].partition_broadcast(128))],
                    writes=["GB%d" % typ])
                op("dve", lambda e, typ=typ: e.tensor_tensor(GB[:, typ, :], GB[:, typ, :], tmpw[:], ALU.mult),
                   reads=["GB%d" % typ, "tmpw"], writes=["GB%d" % typ])

        def norm_transpose(t, hT, col0, xsbuf, xsname, hname, psb):
            typ = 0 if t < 16 else 1
            xs = xsbuf
            op("act", lambda e: e.activation(xs[:], X[:, t, :], AF.Square, accum_out=ssq[:, 0:1]),
               reads=["X%d" % t], writes=[xsname, "ssq"])
            op("act", lambda e: e.activation(ssq[:, 1:2], ssq[:, 0:1], AF.Sqrt, scale=1.0 / D, bias=EPS),
               reads=["ssq"], writes=["ssq"])
            op("dve", lambda e: e.reciprocal(ssq[:, 2:3], ssq[:, 1:2]), reads=["ssq"], writes=["ssq"])
            op("dve", lambda e: e.tensor_scalar(xs[:], X[:, t, :], ssq[:, 2:3], None, ALU.mult),
               reads=["X%d" % t, "ssq"], writes=[xsname])
            for half in range(2):
                pb = bank(psb + half)
                for k4 in range(4):
                    kc = half * 4 + k4
                    op("pe", lambda e, kc=kc, k4=k4, pb=pb: e.transpose(
                        pb[:, k4 * 128:(k4 + 1) * 128], xs[:, kc * 128:(kc + 1) * 128], IDf),
                       reads=[xsname, "CF"], writes=["psnt%d" % (psb + half)])
                for k4 in range(4):
                    kc = half * 4 + k4
                    if kc % 2 == 0:
                        op("dve", lambda e, kc=kc, k4=k4, pb=pb: e.tensor_scalar(
                            hT[:, kc, col0:col0 + 128], pb[:, k4 * 128:(k4 + 1) * 128],
                            AT[:, typ, kc:kc + 1], ST[:, typ, kc:kc + 1], ALU.mult, ALU.add),
                           reads=["psnt%d" % (psb + half), "AT", "ST"], writes=[hname])
                    else:
                        op("act", lambda e, kc=kc, k4=k4, pb=pb: e.activation(
                            hT[:, kc, col0:col0 + 128], pb[:, k4 * 128:(k4 + 1) * 128], AF.Identity,
                            bias=ST[:, typ, kc:kc + 1], scale=AT[:, typ, kc:kc + 1]),
                           reads=["psnt%d" % (psb + half), "AT", "ST"], writes=[hname])

        def post_residual(t, psb, psname, tmp, tmpname, GB, slot=0):
            typ = 0 if t < 16 else 1
            psn = psname if isinstance(psname, list) else [psname, psname]
            sq = ssq[:, 4 * slot:4 * slot + 4]
            sqn = "ssq" if slot == 0 else "ssq%d" % slot
            for n in range(2):
                op("act", lambda e, n=n: e.activation(tmp[:, n * 512:(n + 1) * 512], bank(psb + n), AF.Square,
                                                      accum_out=sq[:, 3 * n:3 * n + 1]),
                   reads=[psn[n]], writes=[tmpname, sqn])
            op("dve", lambda e: e.tensor_tensor(sq[:, 0:1], sq[:, 0:1], sq[:, 3:4], ALU.add), reads=[sqn], writes=[sqn])
            op("act", lambda e: e.activation(sq[:, 1:2], sq[:, 0:1], AF.Sqrt, scale=1.0 / D, bias=EPS),
               reads=[sqn], writes=[sqn])
            op("dve", lambda e: e.reciprocal(sq[:, 2:3], sq[:, 1:2]), reads=[sqn], writes=[sqn])
            for n in range(2):
                op("dve", lambda e, n=n: e.scalar_tensor_tensor(
                    tmp[:, n * 512:(n + 1) * 512], bank(psb + n), sq[:, 2:3], GB[:, typ, n * 512:(n + 1) * 512],
                    ALU.mult, ALU.mult),
                   reads=[psn[n], sqn, "GB%d" % typ], writes=[tmpname])
            op("pool", lambda e: e.tensor_tensor(X[:, t, :], X[:, t, :], tmp[:], ALU.add),
               reads=["X%d" % t, tmpname], writes=["X%d" % t])

        def dump(name, ap_sb, bufname, rows=128):
            if dbg and name in dbg_d:
                dma("pool", lambda e: [e.dma_start(out=dbg_d[name], in_=ap_sb)], reads=[bufname], dbuf="dump_" + name)

        for l in range(depth if stop not in ("s0", "s0a", "s0b", "s0c", "s0d", "s0e", "init") else 0):
            last = (l == DEPTH - 1)
            ntile = 16 if last else 18
            P.barrier()
            A.release(small_mark)
            prep_at(l, 0)
            m1 = A.mark()
            QT = A.alloc("QT", [128, 4, NTOK], BF16)
            KT2 = A.alloc("KT2", [128, 2, NTOK], BF16)
            VA = A.alloc("VA", [128, NT, 2, 192], BF16)
            dtraw = A.alloc("dtraw", [128, NT, 8], F32)
            m1b = A.mark()
            WIN = A.alloc("WIN", [128, 8, IN_DIM], BF16)
            hTg = [A.alloc("hTg%d" % i, [128, 8, 512], BF16) for i in range(2)]
            xsb = [A.alloc("xsb%d" % i, [128, D], F32) for i in range(1)]
            qn2 = [A.alloc("qn%d" % i, [128, 10, 64], F32) for i in range(2)]
            rt2 = [[A.alloc("rt%d_%d" % (j, i), [128, 10, 32], F32) for i in range(4)] for j in range(2)]
            qkr2 = [A.alloc("qkr%d" % i, [128, 10, 64], F32) for i in range(2)]
            kd2 = [A.alloc("kd%d" % i, [128, 2, 2, 64], F32) for i in range(2)]
            rs2 = [A.alloc("rs%d" % i, [128, 32], F32) for i in range(2)]
            gq = A.alloc("gq", [128, 64], F32)
            gk = A.alloc("gk", [128, 64], F32)
            szt = [A.alloc("szt%d" % i, [128, 512], BF16) for i in range(1)]
            xbt = [A.alloc("xbt%d" % i, [128, 512], BF16) for i in range(2)]
            COS = A.alloc("COS", [128, NT, 32], F32)
            SIN = A.alloc("SIN", [128, NT, 32], F32)
            dma("sp", lambda e: [e.dma_start(out=COS[:], in_=c_cos)], writes=["COS"])
            dma("sp", lambda e: [e.dma_start(out=SIN[:], in_=c_sin)], writes=["SIN"])
            dma("pool", lambda e: [e.dma_start(out=WIN[:, kc, :], in_=w_in[l, kc * 128:(kc + 1) * 128, :])
                                   for kc in range(8)], writes=["WIN"], n_dma=8)
            dma("sp", lambda e: [e.dma_start(out=gq[:], in_=q_norm[l:l + 1, :].partition_broadcast(128))], writes=["gq"])
            dma("sp", lambda e: [e.dma_start(out=gk[:], in_=k_norm[l:l + 1, :].partition_broadcast(128))], writes=["gk"])
            op("pool", lambda e: e.memset(VA[:], 1.0), writes=["VA"])
            groups = [[0, 1, 2, 3], [4, 5, 6, 7], [8, 9, 10, 11], [12, 13, 14, 15], [16, 17]]
            if stop in ("s1a", "s1b", "s1c", "s1b1", "s1b2", "s1b3"):
                groups = groups[:1]
            lim = {"s1b1": 1, "s1b2": 2, "s1b3": 3}.get(stop, 9)
            for ti, t in enumerate(groups[0]):
                norm_transpose(t, hTg[0], ti * 128, xsb[0], "xsb0", "hTg0", 0)
            for gi, tiles in enumerate(groups):
                hs = gi % 2
                hT = hTg[hs]
                ntk = len(tiles) * 128
                tok0 = tiles[0] * 128
                nxt_tiles = groups[gi + 1] if gi + 1 < len(groups) else []
                def tile_mm(ti, t):
                    cs = slice(ti * 128, (ti + 1) * 128)
                    st_ = t % 2
                    bq, bkv, bz = (2, 3, 4) if st_ == 0 else (5, 6, 7)
                    nq_, nkv_, nz_ = "pb%d" % bq, "pb%d" % bkv, "pb%d" % bz
                    qn, rt, qkr, kd, rs = qn2[st_], rt2[st_], qkr2[st_], kd2[st_], rs2[st_]
                    qnn, qkrn, kdn, rsn = "qn%d" % st_, "qkr%d" % st_, "kd%d" % st_, "rs%d" % st_
                    rtn = ["rt%d_%d" % (st_, i) for i in range(4)]
                    for kc in range(8):
                        f, la = (kc == 0), (kc == 7)
                        rd = ["hTg%d" % hs, "WIN"]
                        op("pe", lambda e: e.matmul(bank(bq), hT[:, kc, cs], WIN[:, kc, 0:512], start=f, stop=la),
                           reads=rd, writes=[nq_])
                        op("pe", lambda e: e.matmul(bank(bkv)[:, 0:256], hT[:, kc, cs], WIN[:, kc, 512:768], start=f, stop=la),
                           reads=rd, writes=[nkv_])
                        op("pe", lambda e: e.matmul(bank(bz), hT[:, kc, cs], WIN[:, kc, 768:1280], start=f, stop=la),
                           reads=rd, writes=[nz_])
                    for kc in range(8):
                        op("pe", lambda e: e.matmul(bank(bkv)[:, 256:264], hT[:, kc, cs], WIN[:, kc, 2304:2312],
                                                    start=(kc == 0), stop=(kc == 7)),
                           reads=["hTg%d" % hs, "WIN"], writes=[nkv_])

                def tile_s1(ti, t):
                    cs = slice(ti * 128, (ti + 1) * 128)
                    st_ = t % 2
                    bq, bkv, bz = (2, 3, 4) if st_ == 0 else (5, 6, 7)
                    nq_, nkv_, nz_ = "pb%d" % bq, "pb%d" % bkv, "pb%d" % bz
                    qn, rt, qkr, kd, rs = qn2[st_], rt2[st_], qkr2[st_], kd2[st_], rs2[st_]
                    qnn, qkrn, kdn, rsn = "qn%d" % st_, "qkr%d" % st_, "kd%d" % st_, "rs%d" % st_
                    rtn = ["rt%d_%d" % (st_, i) for i in range(4)]
                    sqv = qkr[:].rearrange("p h d -> p (h d)")
                    op("act", lambda e: e.activation(sqv[:, 0:512], bank(bq), AF.Square), reads=[nq_], writes=[qkrn])
                    op("act", lambda e: e.activation(sqv[:, 512:640], bank(bkv)[:, 0:128], AF.Square), reads=[nkv_], writes=[qkrn])
                    op("dve", lambda e: e.tensor_reduce(rs[:, 0:10], qkr[:], AX.X, ALU.add), reads=[qkrn], writes=[rsn])
                    op("act", lambda e: e.activation(rs[:, 10:20], rs[:, 0:10], AF.Sqrt, scale=1.0 / 64, bias=EPS),
                       reads=[rsn], writes=[rsn])
                    op("dve", lambda e: e.reciprocal(rs[:, 20:30], rs[:, 10:20]), reads=[rsn], writes=[rsn])
                    op("dve", lambda e: e.tensor_tensor(
                        qn[:, 0:8, :], bank(bq).rearrange("p (h d) -> p h d", d=64),
                        rs[:, 20:28].unsqueeze(2).to_broadcast([128, 8, 64]), ALU.mult),
                       reads=[nq_, rsn], writes=[qnn])
                    op("dve", lambda e: e.tensor_tensor(
                        qn[:, 8:10, :], bank(bkv)[:, 0:128].rearrange("p (h d) -> p h d", d=64),
                        rs[:, 28:30].unsqueeze(2).to_broadcast([128, 2, 64]), ALU.mult),
                       reads=[nkv_, rsn], writes=[qnn])
                    for g in range(2):
                        op("act", lambda e: e.copy(VA[:, t, g, 64:128], bank(bkv)[:, 128 + g * 64:192 + g * 64]),
                           reads=[nkv_], writes=["VA"])
                    op("act", lambda e: e.activation(szt[0][:], bank(bz), AF.Silu), reads=[nz_], writes=["szt0"])
                    dma("sp", lambda e: [e.dma_start(out=sz_d[t * 128:(t + 1) * 128, :], in_=szt[0][:])], reads=["szt0"])
                    op("dve", lambda e: e.tensor_copy(dtraw[:, t, :], bank(bkv)[:, 256:264]), reads=[nkv_], writes=["dtraw"])
                    op("pool", lambda e: e.tensor_tensor(qn[:, 0:8, :], qn[:, 0:8, :],
                                                         gq[:].unsqueeze(1).to_broadcast([128, 8, 64]), ALU.mult),
                       reads=[qnn, "gq"], writes=[qnn])
                    op("pool", lambda e: e.tensor_tensor(qn[:, 8:10, :], qn[:, 8:10, :],
                                                         gk[:].unsqueeze(1).to_broadcast([128, 2, 64]), ALU.mult),
                       reads=[qnn, "gk"], writes=[qnn])
                    cosb = COS[:, t, :].unsqueeze(1).to_broadcast([128, 10, 32])
                    sinb = SIN[:, t, :].unsqueeze(1).to_broadcast([128, 10, 32])
                    x1, x2 = qn[:, :, 0:32], qn[:, :, 32:64]
                    op("dve", lambda e: e.tensor_tensor(rt[0][:], x1, cosb, ALU.mult), reads=[qnn, "COS"], writes=[rtn[0]])
                    op("pool", lambda e: e.tensor_tensor(rt[1][:], x2, sinb, ALU.mult), reads=[qnn, "SIN"], writes=[rtn[1]])
                    op("dve", lambda e: e.tensor_tensor(qkr[:, :, 0:32], rt[0][:], rt[1][:], ALU.subtract),
                       reads=[rtn[0], rtn[1]], writes=[qkrn])
                    op("pool", lambda e: e.tensor_tensor(rt[2][:], x1, sinb, ALU.mult), reads=[qnn, "SIN"], writes=[rtn[2]])
                    op("dve", lambda e: e.tensor_tensor(rt[3][:], x2, cosb, ALU.mult), reads=[qnn, "COS"], writes=[rtn[3]])
                    op("dve", lambda e: e.tensor_tensor(qkr[:, :, 32:64], rt[2][:], rt[3][:], ALU.add),
                       reads=[rtn[2], rtn[3]], writes=[qkrn])

                def tile_s2(ti, t):
                    cs = slice(ti * 128, (ti + 1) * 128)
                    st_ = t % 2
                    bq, bkv, bz = (2, 3, 4) if st_ == 0 else (5, 6, 7)
                    nq_, nkv_, nz_ = "pb%d" % bq, "pb%d" % bkv, "pb%d" % bz
                    qn, rt, qkr, kd, rs = qn2[st_], rt2[st_], qkr2[st_], kd2[st_], rs2[st_]
                    qnn, qkrn, kdn, rsn = "qn%d" % st_, "qkr%d" % st_, "kd%d" % st_, "rs%d" % st_
                    rtn = ["rt%d_%d" % (st_, i) for i in range(4)]
                    for r in range(2):
                        op("dve", lambda e: e.tensor_copy(kd[:, :, r, :], qkr[:, 8:10, :]), reads=[qkrn], writes=[kdn])
                    for pr in range(4):
                        op("pe", lambda e: e.transpose(bank(0)[:, pr * 128:(pr + 1) * 128],
                                                       qkr[:, 2 * pr:2 * pr + 2, :].rearrange("p h d -> p (h d)"), IDf),
                           reads=[qkrn, "CF"], writes=["psnt0"])
                    for g in range(2):
                        op("pe", lambda e: e.transpose(bank(1)[:, g * 128:(g + 1) * 128],
                                                       kd[:, g].rearrange("p r d -> p (r d)"), IDf),
                           reads=[kdn, "CF"], writes=["psnt1"])
                    for pr in range(4):
                        op("act", lambda e: e.copy(QT[:, pr, t * 128:(t + 1) * 128], bank(0)[:, pr * 128:(pr + 1) * 128]),
                           reads=["psnt0"], writes=["QT"])
                    for g in range(2):
                        op("dve", lambda e: e.tensor_copy(KT2[:, g, t * 128:(t + 1) * 128], bank(1)[:, g * 128:(g + 1) * 128]),
                           reads=["psnt1"], writes=["KT2"])
                    if ti < len(nxt_tiles):
                        norm_transpose(nxt_tiles[ti], hTg[1 - hs], ti * 128, xsb[0], "xsb0", "hTg%d" % (1 - hs), 0)

                tl_ = tiles if stop != "s1a" else []
                if tl_:
                    tile_mm(0, tl_[0])
                    tile_s1(0, tl_[0])
                for ti, t in enumerate(tl_):
                    if ti + 1 < len(tl_):
                        tile_mm(ti + 1, tl_[ti + 1])
                        tile_s1(ti + 1, tl_[ti + 1])
                    tile_s2(ti, t)
                P.mute = False
                for ti in range(len(tiles), len(nxt_tiles)):
                    norm_transpose(nxt_tiles[ti], hTg[1 - hs], ti * 128, xsb[0], "xsb0", "hTg%d" % (1 - hs), 0)
                for fc in range(8 if stop not in ("s1a", "s1b", "s1b1", "s1b2", "s1b3") else 0):
                    pb = 5 + (fc % 2)
                    pbn = "pb%d" % pb
                    for kc in range(8):
                        op("pe", lambda e, kc=kc, fc=fc, pb=pb: e.matmul(
                            bank(pb)[:, 0:ntk], WIN[:, kc, 1280 + fc * 128:1280 + (fc + 1) * 128], hT[:, kc, 0:ntk],
                            start=(kc == 0), stop=(kc == 7)),
                           reads=["hTg%d" % hs, "WIN"], writes=[pbn])
                    xs_ = fc % 2
                    if fc % 2 == 0:
                        op("dve", lambda e, pb=pb, xs_=xs_: e.tensor_copy(xbt[xs_][:, 0:ntk], bank(pb)[:, 0:ntk]),
                           reads=[pbn], writes=["xbt%d" % xs_])
                    else:
                        op("act", lambda e, pb=pb, xs_=xs_: e.copy(xbt[xs_][:, 0:ntk], bank(pb)[:, 0:ntk]),
                           reads=[pbn], writes=["xbt%d" % xs_])
                    dma("sp", lambda e, fc=fc, xs_=xs_: [e.dma_start(
                        out=raw_d[fc * 128:(fc + 1) * 128, tok0:tok0 + ntk], in_=xbt[xs_][:, 0:ntk])],
                        reads=["xbt%d" % xs_])
            if dbg:
                dump("QT", QT[:, 0, 0:512], "QT")
            P.barrier()
            A.release(m1b)
            if stop in ("s1", "s1a", "s1b", "s1c", "s1b1", "s1b2", "s1b3"):
                break

            ATT = A.alloc("ATT", [128, 4, NTOK], BF16)
            m3 = A.mark()
            PT = [A.alloc("PT%d" % i, [128, 1024], BF16) for i in range(3)]
            rec = [A.alloc("rec%d" % i, [128, 1024], F32) for i in range(1)]
            ocp = [A.alloc("ocp%d" % i, [128, 1024], F32) for i in range(2)]
            KTZ = A.alloc("KTZ", [128, 2, 2, NTOK], BF16)
            op("pool", lambda e: e.memset(KTZ[:], 0.0), writes=["KTZ"])
            for g in range(2):
                for par in range(2):
                    rows = slice(par * 64, par * 64 + 64)
                    op("dve" if par == 0 else "act",
                       (lambda e: e.tensor_copy(KTZ[rows, g, par, :], KT2[rows, g, :])) if par == 0 else
                       (lambda e: e.copy(KTZ[rows, g, par, :], KT2[rows, g, :])),
                       reads=["KT2", "KTZ"], writes=["KTZ"])
            for f in range(NF):
                wv = wsc_d[f].rearrange("p (kc w n) -> p kc w n", kc=8, w=2)
                dma("pool", lambda e: [
                    e.dma_start(out=wv[:, :, 0, :], in_=w_gate[l][:, f * 128:(f + 1) * 128].rearrange("(kc p) n -> p kc n", p=128)),
                    e.dma_start(out=wv[:, :, 1, :], in_=w_up[l][:, f * 128:(f + 1) * 128].rearrange("(kc p) n -> p kc n", p=128))],
                    writes=["wscd%d" % f], n_dma=2, dbuf="wscpre%d" % (f % 4))
            qsets = [(0, 1024, list(range(NT))), (1024, 1024, list(range(NT)))]
            if not last:
                qsets.append((NLAT, 256, [16, 17]))
            seq = []
            gi_ = 0
            for h in range(8):
                for (q0, nq, kts) in qsets:
                    for ki, kt in enumerate(kts):
                        seq.append((h, q0, nq, ki, kt, len(kts), gi_))
                    gi_ += 1

            def emit_s(i):
                h, q0, nq, ki, kt, nk, gi = seq[i]
                g, pr, par = h // 4, h // 2, h % 2
                sl = i % 3
                for j in range((nq + 511) // 512):
                    w = min(512, nq - j * 512)
                    op("pe", lambda e: e.matmul(
                        bank(2 * sl + j)[:, 0:w], KTZ[:, g, par, kt * 128:(kt + 1) * 128],
                        QT[:, pr, q0 + j * 512:q0 + j * 512 + w], start=True, stop=True),
                       reads=["KTZ", "QT"], writes=["pss%d" % sl])

            def emit_rest(i):
                h, q0, nq, ki, kt, nk, gi = seq[i]
                g, pr, par = h // 4, h // 2, h % 2
                orow = slice(par * 64, par * 64 + 64)
                srow = slice((1 - par) * 64, (1 - par) * 64 + 64)
                vcols = slice(64, 192) if par == 0 else slice(0, 128)
                sl = i % 3
                nch = (nq + 511) // 512
                op("act", lambda e: e.activation(PT[sl][:, 0:nq], bank(2 * sl, 2)[:, 0:nq], AF.Exp, scale=0.125),
                   reads=["pss%d" % sl], writes=["PT%d" % sl])
                for j in range(nch):
                    w = min(512, nq - j * 512)
                    op("pe", lambda e: e.matmul(
                        bank(6 + j)[:, 0:w], VA[:, kt, g, vcols], PT[sl][:, j * 512:j * 512 + w],
                        start=(ki == 0), stop=(ki == nk - 1)),
                       reads=["VA", "PT%d" % sl], writes=["pso"])
                if ki == nk - 1:
                    oc = ocp[gi % 2]
                    ocn = "ocp%d" % (gi % 2)
                    for j in range(nch):
                        w = min(512, nq - j * 512)
                        op("dve", lambda e: e.tensor_copy(oc[:, j * 512:j * 512 + w], bank(6 + j)[:, 0:w]),
                           reads=["pso"], writes=[ocn])
                    op("dve", lambda e: e.reciprocal(rec[0][orow, 0:nq], oc[srow, 0:nq]), reads=[ocn], writes=["rec0"])
                    op("dve", lambda e: e.tensor_tensor(ATT[orow, pr, q0:q0 + nq], oc[orow, 0:nq], rec[0][orow, 0:nq], ALU.mult),
                       reads=[ocn, "rec0"], writes=["ATT"])

            emit_s(0)
            emit_s(1)
            for i in range(len(seq)):
                if i + 2 < len(seq):
                    emit_s(i + 2)
                emit_rest(i)
            if dbg:
                dump("ATT", ATT[:, 0, 0:512], "ATT")
            P.barrier()
            A.release(m3)
            if dbg and dbg.get("_stop") == "attn":
                break

            m_att = A.mark()
            A.release(m1)
            XBT = A.alloc("XBT", [128, 8, NTOK], BF16)
            assert A.mark() <= m1 + 18432 + 9216 + 13824
            A.release(m1)
            HBI = A.alloc("HBI", [128, NT, 512], BF16)
            A.release(m_att)
            XST = A.alloc("XST", [128, NT, 512], BF16)
            BTK = A.alloc("BTK", [128, NT, 256], BF16)
            m2 = A.mark()
            rawb = [A.alloc("rawb%d" % i, [128, NTOK], BF16) for i in range(2)]
            accs = [A.alloc("acc%d" % i, [128, NTOK], F32) for i in range(2)]
            cw = A.alloc("cw", [128, 8, 5], F32)
            cb = A.alloc("cb", [128, 8], F32)
            dma("sp", lambda e: [e.dma_start(out=cw[:, :, k_], in_=conv_w[l, k_].rearrange("(fc p) -> p fc", p=128),
                                             allow_slow_non_contiguous=True) for k_ in range(5)],
                writes=["cw"], n_dma=5)
            dma("sp", lambda e: [e.dma_start(out=cb[:], in_=conv_b[l].rearrange("(fc p) -> p fc", p=128),
                                             allow_slow_non_contiguous=True)], writes=["cb"])
            def tok_major(t, which):
                tcs = slice(t * 128, (t + 1) * 128)
                if which == 0:
                    pbx = 2 * (t % 2)
                    for k4 in range(4):
                        op("pe", lambda e: e.matmul(bank(pbx)[:, k4 * 128:(k4 + 1) * 128], XBT[:, k4, tcs], IDb[:],
                                                    start=True, stop=True),
                           reads=["XBT%d" % k4, "IDb"], writes=["pb%d" % pbx])
                    op("act", lambda e: e.copy(XST[:, t, :], bank(pbx)), reads=["pb%d" % pbx], writes=["XST"])
                else:
                    pbb = 2 * (t % 2) + 1
                    for g in range(2):
                        op("pe", lambda e: e.matmul(bank(pbb)[:, g * 128:(g + 1) * 128], XBT[:, 4 + g, tcs], IDb[:],
                                                    start=True, stop=True),
                           reads=["XBT%d" % (4 + g), "IDb"], writes=["pb%d" % pbb])
                    op("act", lambda e: e.copy(BTK[:, t, :], bank(pbb)[:, 0:256]), reads=["pb%d" % pbb], writes=["BTK"])

            for fc in range(8):
                s = fc % 2
                rb = rawb[s]
                rbn = "rawb%d" % s
                acc = accs[s]
                an = "acc%d" % s
                dma("sp", lambda e: [e.dma_start(out=rb[:], in_=raw_d[fc * 128:(fc + 1) * 128, :])], writes=[rbn])
                for (a0, b0) in ((0, NLAT), (NLAT, NTOK)):
                    op("dve", lambda e: e.tensor_scalar(acc[:, a0:b0], rb[:, a0:b0], cw[:, fc, 2:3], None, ALU.mult),
                       reads=[rbn, "cw"], writes=[an])
                    for (kk, do, si) in ((0, 2, 0), (1, 1, 0), (3, 0, 1), (4, 0, 2)):
                        n_ = (b0 - a0) - max(do, si)
                        op("dve", lambda e: e.scalar_tensor_tensor(
                            acc[:, a0 + do:a0 + do + n_], rb[:, a0 + si:a0 + si + n_], cw[:, fc, kk:kk + 1],
                            acc[:, a0 + do:a0 + do + n_], ALU.mult, ALU.add),
                           reads=[rbn, "cw", an], writes=[an])
                op("act", lambda e: e.activation(XBT[:, fc, :], acc[:], AF.Silu, bias=cb[:, fc:fc + 1]),
                   reads=[an, "cb"], writes=["XBT%d" % fc])
                if fc == 4:
                    for t in range(NT):
                        tok_major(t, 0)
                if fc == 7:
                    for t in range(NT):
                        tok_major(t, 1)
            if dbg:
                dump("XBT", XBT[:, 0, 0:512], "XBT0")
            P.barrier()
            A.release(m2)
            if stop == "conv":
                break

            sma = lambda nm: A.alloc(nm, [128, NT, 8], F32)
            a_d = [sma("a_f"), sma("a_b")]
            w_d = [sma("w_f"), sma("w_b")]
            ea_d = [sma("ea_f"), sma("ea_b")]
            cd_d = [sma("cd_f"), sma("cd_b")]
            de_d = [sma("de_f"), sma("de_b")]
            tmpa = sma("tmpa")
            prm = A.alloc("prm", [128, 5, 8], F32)
            nal = A.alloc("nal", [128, 2, 8], F32)
            ssdw = A.alloc("ssdw", [128, 512], F32)
            aU = [A.alloc("aU%d" % i, [128, 1024], F32) for i in range(2)]
            Ed = [A.alloc("E%d" % i, [128, 1024], BF16) for i in range(2)]
            MT = A.alloc("MT", [128, 1024], BF16)
            GTs = A.alloc("GTs", [128, 256], F32)
            t1 = A.alloc("t1", [128, 512], F32)
            t2 = A.alloc("t2", [128, 512], F32)
            ys = A.alloc("ys", [128, 512], F32)
            gout = [A.alloc("gout%d" % i, [128, 512], BF16) for i in range(1)] * 2
            xw = A.alloc("xw", [128, 512], BF16)
            xw_b = A.alloc("xw_b", [128, 512], BF16)
            Hs = A.alloc("Hs", [128, 512], F32)
            Hfb = A.alloc("Hfb", [128, 512], BF16)
            szc = [A.alloc("szc%d" % i, [128, 512], BF16) for i in range(1)] * 2
            P.region = True
            P.limit = (dbg or {}).get("_lim", 10 ** 9)
            for i, src in enumerate((dtb_f, dtb_b, alog_f, alog_b, d_skip)):
                dma("sp", lambda e: [e.dma_start(out=prm[:, i, :], in_=src[l:l + 1, :].partition_broadcast(128))],
                    writes=["prm%d" % i])
            dma("sp", lambda e: [e.dma_start(out=ssdw[:], in_=ssd_norm[l:l + 1, :].partition_broadcast(128))],
                writes=["ssdw"])
            bc38 = lambda ap: ap.unsqueeze(1).to_broadcast([128, NT, 8])
            flat = lambda tns: tns[:].rearrange("p c h -> p (c h)")
            for d in range(2):
                dn = "fb"[d]
                op("dve", lambda e: e.tensor_tensor(tmpa[:], dtraw[:], bc38(prm[:, d, :]), ALU.add),
                   reads=["dtraw", "prm%d" % d], writes=["tmpa"])
                op("act", lambda e: e.activation(tmpa[:], tmpa[:], AF.Exp), reads=["tmpa"], writes=["tmpa"])
                op("act", lambda e: e.activation(tmpa[:], tmpa[:], AF.Ln, bias=1.0), reads=["tmpa"], writes=["tmpa"])
                op("act", lambda e: e.activation(w_d[d][:], tmpa[:], AF.Ln), reads=["tmpa"], writes=["w_" + dn])
                op("act", lambda e: e.activation(nal[:, d, :], prm[:, 2 + d, :], AF.Exp), reads=["prm%d" % (2 + d)], writes=["nal"])
                op("dve", lambda e: e.scalar_tensor_tensor(a_d[d][:], tmpa[:], -1.0, bc38(nal[:, d, :]), ALU.mult, ALU.mult),
                   reads=["tmpa", "nal"], writes=["a_" + dn])
                op("pe", lambda e: e.matmul(bank(d)[:, 0:NT * 8], Uf if d == 0 else UTf, flat(a_d[d]), start=True, stop=True),
                   reads=["a_" + dn, "CF"], writes=["pb%d" % d])
                op("pe", lambda e: e.matmul(bank(2 + d)[:, 0:NT * 8], ONf, flat(a_d[d]), start=True, stop=True),
                   reads=["a_" + dn, "CF"], writes=["pb%d" % (2 + d)])
                op("dve", lambda e: e.tensor_tensor(flat(w_d[d]), flat(w_d[d]), bank(d)[:, 0:NT * 8], ALU.subtract),
                   reads=["w_" + dn, "pb%d" % d], writes=["w_" + dn])
                op("act", lambda e: e.activation(flat(ea_d[d]), bank(d)[:, 0:NT * 8], AF.Exp), reads=["pb%d" % d], writes=["ea_" + dn])
                op("act", lambda e: e.activation(flat(cd_d[d]), bank(2 + d)[:, 0:NT * 8], AF.Exp), reads=["pb%d" % (2 + d)], writes=["cd_" + dn])
                op("dve", lambda e: e.tensor_tensor(flat(tmpa), flat(w_d[d]), bank(2 + d)[:, 0:NT * 8], ALU.add),
                   reads=["w_" + dn, "pb%d" % (2 + d)], writes=["tmpa"])
                op("act", lambda e: e.activation(de_d[d][:], tmpa[:], AF.Exp), reads=["tmpa"], writes=["de_" + dn])
            P.region = False
            if stop == "ssd0":
                P.mute = True
            h3 = lambda ap: ap.rearrange("p (h q) -> p h q", q=64)
            bh = lambda ap: ap.unsqueeze(2).to_broadcast([128, 8, 64])

            def state_update(c, d, hbuf, hname, pbs=0):
                dn = "fb"[d]
                op("dve", lambda e: e.tensor_tensor(h3(xw[:]), h3(XST[:, c, :]), bh(de_d[d][:, c, :]), ALU.mult),
                   reads=["XST", "de_" + dn], writes=["xw"])
                for g in range(2):
                    op("pe", lambda e: e.matmul(bank(pbs)[:, g * 256:(g + 1) * 256], BTK[:, c, g * 128:(g + 1) * 128],
                                                xw[:, g * 256:(g + 1) * 256], start=True, stop=True),
                       reads=["BTK", "xw"], writes=["pb%d" % pbs])
                op("dve", lambda e: e.tensor_tensor(h3(hbuf[:]), h3(hbuf[:]), bh(cd_d[d][:, c, :]), ALU.mult),
                   reads=[hname, "cd_" + dn], writes=[hname])
                op("dve", lambda e: e.tensor_tensor(hbuf[:], hbuf[:], bank(pbs), ALU.add),
                   reads=[hname, "pb%d" % pbs], writes=[hname])

            op("pool", lambda e: e.memset(Hs[:], 0.0), writes=["Hs"])
            order1 = [17, 16] + list(range(15, -1, -1))

            def p1_mm(i):
                c = order1[i]
                xb, xbn, pbs = (xw, "xw", 0) if i % 2 == 0 else (xw_b, "xw_b", 7)
                op("dve", lambda e: e.tensor_tensor(h3(xb[:]), h3(XST[:, c, :]), bh(de_d[1][:, c, :]), ALU.mult),
                   reads=["XST", "de_b"], writes=[xbn])
                for g in range(2):
                    op("pe", lambda e: e.matmul(bank(pbs)[:, g * 256:(g + 1) * 256], BTK[:, c, g * 128:(g + 1) * 128],
                                                xb[:, g * 256:(g + 1) * 256], start=True, stop=True),
                       reads=["BTK", xbn], writes=["pb%d" % pbs])

            def p1_upd(i):
                c = order1[i]
                pbs = 0 if i % 2 == 0 else 7
                op("act", lambda e: e.copy(HBI[:, c, :], Hs[:]), reads=["Hs"], writes=["HBI"])
                op("dve", lambda e: e.tensor_tensor(h3(Hs[:]), h3(Hs[:]), bh(cd_d[1][:, c, :]), ALU.mult),
                   reads=["Hs", "cd_b"], writes=["Hs"])
                op("dve", lambda e: e.tensor_tensor(Hs[:], Hs[:], bank(pbs), ALU.add),
                   reads=["Hs", "pb%d" % pbs], writes=["Hs"])

            p1_mm(0)
            for i in range(NT):
                if i + 1 < NT:
                    p1_mm(i + 1)
                p1_upd(i)
            if stop == "ssd1":
                P.mute = True
            op("pool", lambda e: e.memset(Hs[:], 0.0), reads=["Hs"], writes=["Hs"])
            op("pool", lambda e: e.memset(Hfb[:], 0.0), writes=["Hfb"])
            order2 = [16, 17] + list(range(16))

            def part_a1(ci):
                c = order2[ci]
                tcs = slice(c * 128, (c + 1) * 128)
                for g in range(2):
                    op("pe", lambda e: e.matmul(bank(0)[:, g * 128:(g + 1) * 128], XBT[:, 4 + g, tcs], XBT[:, 6 + g, tcs],
                                                start=True, stop=True),
                       reads=["XBT"], writes=["pb0"])
                op("act", lambda e: e.copy(GTs[:], bank(0)[:, 0:256]), reads=["pb0"], writes=["GTs"])
                for d in range(2):
                    dn = "fb"[d]
                    Um = Uf if d == 0 else UTf
                    MK = MKf if d == 0 else MKb
                    op("pool", lambda e: e.tensor_tensor(
                        aU[d][:].rearrange("p (h i) -> p h i", i=128), Um.unsqueeze(1).to_broadcast([128, 8, 128]),
                        a_d[d][:, c, :].unsqueeze(2).to_broadcast([128, 8, 128]), ALU.mult),
                       reads=["CF", "a_" + dn], writes=["aU%d" % d])
                    for hf in range(2):
                        pbn = 1 + 2 * d + hf
                        op("pe", lambda e: e.matmul(bank(pbn), ONf, aU[d][:, hf * 512:(hf + 1) * 512], start=True, stop=False),
                           reads=["CF", "aU%d" % d], writes=["pb%d" % pbn])
                        op("pe", lambda e: e.matmul(bank(pbn), IDb[:], MK[:, hf * 512:(hf + 1) * 512], start=False, stop=True),
                           reads=["IDb", "MK" + dn], writes=["pb%d" % pbn])
                    for h in range(8):
                        pbn = 1 + 2 * d + h // 4
                        op("act", lambda e: e.activation(Ed[d][:, h * 128:(h + 1) * 128],
                                                         bank(pbn)[:, (h % 4) * 128:(h % 4 + 1) * 128], AF.Exp,
                                                         bias=w_d[d][:, c, h:h + 1]),
                           reads=["pb%d" % pbn, "w_" + dn], writes=["E%d" % d])

            def part_a2(ci):
                c = order2[ci]
                py = 7
                op("pool", lambda e: e.tensor_tensor(Ed[0][:], Ed[0][:], Ed[1][:], ALU.add), reads=["E0", "E1"], writes=["E0"])
                for g in range(2):
                    op("dve", lambda e: e.tensor_tensor(
                        MT[:, g * 512:(g + 1) * 512].rearrange("p (h i) -> p h i", i=128),
                        Ed[0][:, g * 512:(g + 1) * 512].rearrange("p (h i) -> p h i", i=128),
                        GTs[:, g * 128:(g + 1) * 128].unsqueeze(1).to_broadcast([128, 4, 128]), ALU.mult),
                       reads=["E0", "GTs"], writes=["MT"])
                for h in range(8):
                    op("pe", lambda e: e.matmul(bank(py)[:, h * 64:(h + 1) * 64], MT[:, h * 128:(h + 1) * 128],
                                                XST[:, c, h * 64:(h + 1) * 64], start=True, stop=True),
                       reads=["MT", "XST"], writes=["pb%d" % py])

            def part_b1(ci):
                c = order2[ci]
                tcs = slice(c * 128, (c + 1) * 128)
                py = 7
                zs = ci % 2
                dma("sp", lambda e: [e.dma_start(out=szc[0][:], in_=sz_d[c * 128:(c + 1) * 128, :])], writes=["szc0"])
                op("pool", lambda e: e.tensor_tensor(h3(ys[:]), h3(XST[:, c, :]), bh(prm[:, 4, :]), ALU.mult),
                   reads=["XST", "prm4", "ys"], writes=["ys"])
                for g in range(2):
                    op("pe", lambda e: e.matmul(bank(5)[:, g * 256:(g + 1) * 256], XBT[:, 6 + g, tcs],
                                                Hfb[:, g * 256:(g + 1) * 256], start=True, stop=True),
                       reads=["XBT", "Hfb"], writes=["pb5"])
                    op("pe", lambda e: e.matmul(bank(6)[:, g * 256:(g + 1) * 256], XBT[:, 6 + g, tcs],
                                                HBI[:, c, g * 256:(g + 1) * 256], start=True, stop=True),
                       reads=["XBT", "HBI"], writes=["pb6"])
                op("dve", lambda e: e.tensor_tensor(h3(t1[:]), h3(bank(5)), bh(ea_d[0][:, c, :]), ALU.mult),
                   reads=["pb5", "ea_f"], writes=["t1"])
                op("dve", lambda e: e.tensor_tensor(h3(t2[:]), h3(bank(6)), bh(ea_d[1][:, c, :]), ALU.mult),
                   reads=["pb6", "ea_b"], writes=["t2"])
                op("dve", lambda e: e.tensor_tensor(t1[:], t1[:], t2[:], ALU.add), reads=["t1", "t2"], writes=["t1"])
                op("dve", lambda e: e.tensor_tensor(t1[:], t1[:], ys[:], ALU.add), reads=["t1", "ys"], writes=["t1"])
                op("dve", lambda e: e.tensor_tensor(ys[:], bank(py), t1[:], ALU.add), reads=["pb%d" % py, "t1", "ys"], writes=["ys"])
                op("dve", lambda e: e.tensor_tensor(ys[:], ys[:], szc[0][:], ALU.mult), reads=["ys", "szc0"], writes=["ys"])
                op("dve", lambda e: e.scalar_tensor_tensor(t2[:], ys[:], 1.0, ys[:], ALU.mult, ALU.mult, accum_out=ssq[:, 0:1]),
                   reads=["ys", "t2"], writes=["t2", "ssq"])
                op("act", lambda e: e.activation(ssq[:, 1:2], ssq[:, 0:1], AF.Ln, scale=1.0 / 512, bias=EPS),
                   reads=["ssq"], writes=["ssq"])
                op("act", lambda e: e.activation(ssq[:, 2:3], ssq[:, 1:2], AF.Exp, scale=-0.5),
                   reads=["ssq"], writes=["ssq"])
                op("dve", lambda e: e.scalar_tensor_tensor(gout[zs][:], ys[:], ssq[:, 2:3], ssdw[:], ALU.mult, ALU.mult),
                   reads=["ys", "ssq", "ssdw"], writes=["gout0"])
                dma("sp", lambda e: [e.dma_start(out=g_d[c * 128:(c + 1) * 128, :], in_=gout[zs][:])], reads=["gout0"])
                if dbg and c == 0:
                    dump("G", gout[zs][:], "gout0")

            def part_b2(ci):
                c = order2[ci]
                state_update(c, 0, Hs, "Hs", 0)
                op("act", lambda e: e.copy(Hfb[:], Hs[:]), reads=["Hs"], writes=["Hfb"])

            first = 2 if last else 0
            for ci in range(first):
                part_b2(ci)
            part_a1(first)
            part_a2(first)
            for ci in range(first, NT):
                if ci + 1 < NT:
                    part_a1(ci + 1)
                part_b1(ci)
                if ci + 1 < NT:
                    part_a2(ci + 1)
                part_b2(ci)
            P.mute = False
            P.barrier()
            A.release(m2)
            if stop in ("ssd", "ssd0", "ssd1", "ssd2a", "ssd2b", "ssd2c"):
                break

            A.release(m_att)
            GB = A.alloc("GB", [128, 2, D], F32)
            tmpo = A.alloc("tmpo", [128, D], F32)
            prep_gb(l, 0, GB, tmpo)
            WO = A.alloc("WO", [128, 8, D], BF16)
            tmpo2 = A.alloc("tmpo2", [128, D], F32)
            gtk = [A.alloc("gtk%d" % i, [128, 512], BF16) for i in range(2)]
            gT = [A.alloc("gT%d" % i, [128, 512], BF16) for i in range(2)]
            dma("pool", lambda e: [e.dma_start(out=WO[:], in_=w_out[l].rearrange("(kc p) n -> p kc n", p=128))],
                writes=["WO"])
            def op_tr(t):
                s = t % 2
                dma("sp", lambda e: [e.dma_start(out=gtk[s][:], in_=g_d[t * 128:(t + 1) * 128, :])], writes=["gtk%d" % s])
                for k4 in range(4):
                    op("pe", lambda e: e.matmul(bank(2 + s)[:, k4 * 128:(k4 + 1) * 128], gtk[s][:, k4 * 128:(k4 + 1) * 128],
                                                IDb[:], start=True, stop=True),
                       reads=["gtk%d" % s, "IDb"], writes=["pb%d" % (2 + s)])
                op("act", lambda e: e.copy(gT[s][:], bank(2 + s)), reads=["pb%d" % (2 + s)], writes=["gT%d" % s])

            def op_mm(t):
                s = t % 2
                tcs = slice(t * 128, (t + 1) * 128)
                pm0 = 0 if s == 0 else 4
                for n in range(2):
                    for kc in range(8):
                        lhs = ATT[:, kc, tcs] if kc < 4 else gT[s][:, (kc - 4) * 128:(kc - 3) * 128]
                        op("pe", lambda e: e.matmul(bank(pm0 + n), lhs, WO[:, kc, n * 512:(n + 1) * 512],
                                                    start=(kc == 0), stop=(kc == 7)),
                           reads=["ATT", "gT%d" % s, "WO"], writes=["pbm%d" % s])
                post_residual(t, pm0, "pbm%d" % s, tmpo if s == 0 else tmpo2, "tmpw" if s == 0 else "tmpw2", GB, slot=s)

            op_tr(0)
            for t in range(ntile):
                if t + 1 < ntile:
                    op_tr(t + 1)
                op_mm(t)
            if dbg:
                dump("X1", X[:, 0, :], "X0")
            P.barrier()
            if stop == "oproj":
                break

            A.release(m1)
            prep_at(l, 1)
            GB = A.alloc("GB", [128, 2, D], F32)
            tmpo = A.alloc("tmpo", [128, D], F32)
            prep_gb(l, 1, GB, tmpo)
            WD = A.alloc("WD", [128, NF, D], BF16)
            fT = A.alloc("fT", [128, 8, 768], BF16)
            aT = A.alloc("aT", [128, NF, 768], BF16)
            wgu = [A.alloc("wgu%d" % i, [128, 8, 2, 128], BF16) for i in range(2)]
            sg = [A.alloc("sg%d" % i, [128, 384], F32) for i in range(2)]
            xsf = A.alloc("xsf", [128, D], F32)
            dma("pool", lambda e: [e.dma_start(out=WD[:, f_ * 11:(f_ + 1) * 11, :],
                                               in_=w_down[l, f_ * 1408:(f_ + 1) * 1408, :].rearrange("(f p) n -> p f n", p=128))
                                   for f_ in range(2)], writes=["WD"], n_dma=2)
            tl = list(range(ntile))
            it2 = 0
            grps = [tl[g0:g0 + 6] for g0 in range(0, ntile, 6)]
            for ti, t in enumerate(grps[0]):
                norm_transpose(t, fT, ti * 128, xsf, "xsf", "fT", 0)
            for gi, tiles in enumerate(grps):
                ntk = len(tiles) * 128
                chunks = [(c0, min(c0 + 384, ntk)) for c0 in range(0, ntk, 384)]
                for f in range(NF):
                    s = f % 2
                    dma("sp", lambda e: [e.dma_start(out=wgu[s][:].rearrange("p a b c -> p (a b c)"), in_=wsc_d[f])],
                        writes=["wgu%d" % s])
                    for (c0, c1) in chunks:
                        s2 = it2 % 2
                        it2 += 1
                        pg, pu = 2 + 2 * s2, 3 + 2 * s2
                        for kc in range(8):
                            op("pe", lambda e: e.matmul(bank(pg)[:, 0:c1 - c0], wgu[s][:, kc, 0, :], fT[:, kc, c0:c1],
                                                        start=(kc == 0), stop=(kc == 7)),
                               reads=["wgu%d" % s, "fT"], writes=["pb%d" % pg])
                        for kc in range(8):
                            op("pe", lambda e: e.matmul(bank(pu)[:, 0:c1 - c0], wgu[s][:, kc, 1, :], fT[:, kc, c0:c1],
                                                        start=(kc == 0), stop=(kc == 7)),
                               reads=["wgu%d" % s, "fT"], writes=["pb%d" % pu])
                        op("act", lambda e: e.activation(sg[s2][:, 0:c1 - c0], bank(pg)[:, 0:c1 - c0], AF.Silu),
                           reads=["pb%d" % pg], writes=["sg%d" % s2])
                        op("dve", lambda e: e.tensor_tensor(aT[:, f, c0:c1], sg[s2][:, 0:c1 - c0], bank(pu)[:, 0:c1 - c0], ALU.mult),
                           reads=["sg%d" % s2, "pb%d" % pu], writes=["aT"])
                nxt = grps[gi + 1] if gi + 1 < len(grps) else []
                for ti, t in enumerate(tiles):
                    pd0 = 6 if ti % 2 == 0 else 4
                    for n in range(2):
                        for f in range(NF):
                            op("pe", lambda e: e.matmul(bank(pd0 + n), aT[:, f, ti * 128:(ti + 1) * 128],
                                                        WD[:, f, n * 512:(n + 1) * 512], start=(f == 0), stop=(f == NF - 1)),
                               reads=["aT", "WD"], writes=["pb%d" % (pd0 + n)])
                    if ti < len(nxt):
                        norm_transpose(nxt[ti], fT, ti * 128, xsf, "xsf", "fT", 0)
                    post_residual(t, pd0, ["pb%d" % pd0, "pb%d" % (pd0 + 1)], tmpo, "tmpw", GB, slot=1)
                for ti in range(len(tiles), len(nxt)):
                    norm_transpose(nxt[ti], fT, ti * 128, xsf, "xsf", "fT", 0)
            if dbg:
                dump("X2", X[:, 0, :], "X0")
                dump("XC2", X[:, 16, :], "X16")

        for t in range(16):
            dma("sp", lambda e, t=t: [e.dma_start(out=out_d[t * 128:(t + 1) * 128, :], in_=X[:, t, :])],
                reads=["X%d" % t])
        P.finalize()
    return nc


_CONSTS = None


def kernel(**inputs):
    global _CONSTS
    if _CONSTS is None:
        _CONSTS = host_consts()
    nc = build()
    shared = {k: np.ascontiguousarray(np.asarray(v, dtype=np.float32)) for k, v in inputs.items()
              if k not in ("x", "c", "ctx")}
    shared.update(_CONSTS)
    x = np.asarray(inputs["x"], dtype=np.float32)
    c = np.asarray(inputs["c"], dtype=np.float32)
    ctx = np.asarray(inputs["ctx"], dtype=np.float32)
    in_maps = []
    for b in range(8):
        m = dict(shared)
        m["x"] = np.ascontiguousarray(x[b])
        m["c"] = np.ascontiguousarray(c[b])
        m["ctx"] = np.ascontiguousarray(ctx[b])
        in_maps.append(m)
    res = run_bass_kernel_spmd(nc, in_maps, core_ids=list(range(8)))
    return np.stack([r["out"] for r in res.results], axis=0).astype(np.float32)
```

```python
import contextlib
import numpy as np
import concourse.bass as bass
import concourse.mybir as mybir
from concourse.bass_utils import run_bass_kernel_spmd

F32 = mybir.dt.float32
BF16 = mybir.dt.bfloat16
AF = mybir.ActivationFunctionType
ALU = mybir.AluOpType
AX = mybir.AxisListType

D = 1024
NLAT = 2048
NCTX = 256
NTOK = NLAT + NCTX
NT = NTOK // 128
DEPTH = 2
DFF = 2816
NF = DFF // 128
IN_DIM = 2312
EPS = 1e-6
NEG = -30000.0
FUSE_WAIT = True


class Buf:
    __slots__ = ("name", "w", "r", "dsem", "dcount", "_last_dma")

    def __init__(self, name):
        self.name = name
        self.w = None
        self.r = {}
        self.dsem = None
        self.dcount = 0
        self._last_dma = None


class Op:
    __slots__ = ("eng", "fn", "reads", "writes", "dma", "deps", "need_inc", "ticket",
                 "dbuf", "dval", "n_dma", "barrier")

    def __init__(self, eng, fn, reads, writes, dma=False, dbuf=None, n_dma=1):
        self.eng = eng
        self.fn = fn
        self.reads = reads
        self.writes = writes
        self.dma = dma
        self.deps = []
        self.need_inc = False
        self.ticket = None
        self.dbuf = dbuf
        self.dval = None
        self.n_dma = n_dma
        self.barrier = False


class Rec:
    def __init__(self):
        self.calls = []

    def __getattr__(self, name):
        def f(*a, **kw):
            self.calls.append((name, a, kw))
            return None
        return f


def _bind(fn):
    r = Rec()
    fn(r)
    calls = r.calls

    def replay(E):
        return [getattr(E, name)(*a, **kw) for (name, a, kw) in calls]
    return replay


class Prog:
    ENGS = ("pe", "act", "dve", "pool", "sp")

    def __init__(self):
        self.nc = bass.Bass("TRN2", target_bir_lowering=False)
        self.ops = []
        self.bufs = {}

    def eng(self, name):
        nc = self.nc
        return {"pe": nc.tensor, "act": nc.scalar, "dve": nc.vector, "pool": nc.gpsimd,
                "sp": nc.sync}[name]

    def buf(self, name):
        b = self.bufs.get(name)
        if b is None:
            b = Buf(name)
            self.bufs[name] = b
        return b

    mute = False
    region = False
    limit = 10 ** 9
    count = 0

    def _skip(self):
        if self.mute:
            return True
        if self.region:
            self.count += 1
            return self.count > self.limit
        return False

    def op(self, eng, fn, reads=(), writes=()):
        if self._skip():
            return None
        pr = [b for b in reads if isinstance(b, str) and b[:2] in ("ps", "pb")]
        if pr:
            reads = [b for b in reads if b not in pr]
            writes = list(writes) + [b for b in pr if b not in writes]
        o = Op(eng, _bind(fn), [self.buf(b) if isinstance(b, str) else b for b in reads],
               [self.buf(b) if isinstance(b, str) else b for b in writes])
        self.ops.append(o)
        return o

    def dma(self, eng, fn, reads=(), writes=(), n_dma=1, dbuf=None):
        if self._skip():
            return None
        reads = [self.buf(b) if isinstance(b, str) else b for b in reads]
        writes = [self.buf(b) if isinstance(b, str) else b for b in writes]
        dbuf = self.buf(dbuf) if dbuf is not None else (writes[0] if writes else reads[0])
        o = Op(eng, _bind(fn), reads, writes, dma=True, dbuf=dbuf, n_dma=n_dma)
        self.ops.append(o)
        return o

    def barrier(self):
        o = Op("sp", None, [], [])
        o.barrier = True
        self.ops.append(o)

    def finalize(self):
        nc = self.nc
        last = {e: None for e in self.ENGS}
        dma_bufs_seen = []
        for o in self.ops:
            if o.barrier:
                o.deps = [x for x in last.values() if x is not None]
                for d in o.deps:
                    d.need_inc = True
                o.dval = list(dma_bufs_seen)
                for b in self.bufs.values():
                    b.w = None
                    b.r = {}
                    b._last_dma = None
                continue
            deps = []
            for b in o.reads:
                if b.w is not None:
                    deps.append(b.w)
            for b in o.writes:
                if b.w is not None:
                    deps.append(b.w)
                deps.extend(b.r.values())
            if o.dma:
                prev = o.dbuf._last_dma
                if prev is not None:
                    deps.append(prev)
                o.dbuf._last_dma = o
                if o.dbuf not in dma_bufs_seen:
                    dma_bufs_seen.append(o.dbuf)
            seen = set()
            for d in deps:
                if d is o or id(d) in seen:
                    continue
                seen.add(id(d))
                if (not d.dma) and (not o.dma) and d.eng == o.eng and d.eng == "pe":
                    continue
                o.deps.append(d)
                d.need_inc = True
            rk = ("d", id(o.dbuf)) if o.dma else o.eng
            for b in o.reads:
                b.r[rk] = o
            for b in o.writes:
                b.w = o
                b.r = {}
            if not o.dma:
                last[o.eng] = o
        counts = {e: 0 for e in self.ENGS}
        dbufs = []
        for o in self.ops:
            if o.barrier:
                o.ticket = [(b, b.dcount) for b in o.dval]
                continue
            if o.dma:
                b = o.dbuf
                if b.dsem is None:
                    b.dsem = len(dbufs)
                    dbufs.append(b)
                b.dcount += 16 * o.n_dma
                o.dval = b.dcount
            elif o.need_inc:
                counts[o.eng] += 1
                o.ticket = counts[o.eng]
        self.n_dsem = len(dbufs)
        with contextlib.ExitStack() as st:
            esem = {e: st.enter_context(nc.semaphore("s_" + e)) for e in self.ENGS}
            dsem = [st.enter_context(nc.semaphore("d%d" % i)) for i in range(len(dbufs))]
            waited = {e: {} for e in self.ENGS}

            def do_wait(ename, key, val):
                w = waited[ename]
                if w.get(key, 0) >= val:
                    return
                w[key] = val
                sem = dsem[key[1]] if key[0] == "d" else esem[key[1]]
                self.eng(ename).wait_ge(sem, val)

            for o in self.ops:
                if o.barrier:
                    for e in self.ENGS:
                        for d in o.deps:
                            if d.eng != e:
                                do_wait(e, ("e", d.eng), d.ticket)
                        for b, cnt in o.ticket:
                            do_wait(e, ("d", b.dsem), cnt)
                    continue
                need = {}
                for d in o.deps:
                    if d.dma:
                        key = ("d", d.dbuf.dsem)
                        val = d.dval
                    else:
                        key = ("e", d.eng)
                        val = d.ticket
                    if need.get(key, 0) < val:
                        need[key] = val
                pend = [(key, val) for key, val in need.items() if waited[o.eng].get(key, 0) < val]
                fused = pend.pop() if (pend and FUSE_WAIT and o.eng != "pe") else None
                for key, val in pend:
                    do_wait(o.eng, key, val)
                E = self.eng(o.eng)
                if o.dma:
                    insts = o.fn(E)
                    assert len(insts) == o.n_dma, (len(insts), o.n_dma)
                    for ins in insts:
                        ins.then_inc(dsem[o.dbuf.dsem], 16)
                else:
                    insts = o.fn(E)
                    assert len(insts) == 1
                    if o.need_inc:
                        insts[0].then_inc(esem[o.eng], 1)
                if fused is not None:
                    key, val = fused
                    waited[o.eng][key] = val
                    sem = dsem[key[1]] if key[0] == "d" else esem[key[1]]
                    insts[0]._wait_ge(sem, val)
            for b in dbufs:
                do_wait("sp", ("d", b.dsem), b.dcount)
        return nc


class Arena:
    def __init__(self, nc, base, top):
        self.nc = nc
        self.base = (base + 31) // 32 * 32
        self.top = top
        self.ptr = self.base
        self.n = 0

    def alloc(self, name, shape, dtype):
        nbytes = int(np.prod(shape[1:])) * (2 if dtype == BF16 else 4)
        off = self.ptr
        self.ptr = (off + nbytes + 31) // 32 * 32
        assert self.ptr <= self.top, ("SBUF overflow", name, self.ptr, self.top)
        self.n += 1
        return self.nc.alloc_sbuf_tensor_at("%s_%d" % (name, self.n), list(shape), dtype, offset=off)

    def mark(self):
        return self.ptr

    def release(self, m):
        self.ptr = m


def host_consts():
    t = np.arange(128)
    U = (t[:, None] <= t[None, :]).astype(np.float32)
    UT = (t[:, None] >= t[None, :]).astype(np.float32)
    mf = np.where(t[:, None] <= t[None, :], 0.0, NEG).astype(np.float32)
    mb = np.where(t[:, None] >= t[None, :], 0.0, NEG).astype(np.float32)
    maskf = np.tile(mf[:, None, :], (1, 8, 1)).reshape(128, 1024)
    maskb = np.tile(mb[:, None, :], (1, 8, 1)).reshape(128, 1024)
    ident = np.eye(128, dtype=np.float32)
    ones = np.ones((128, 128), np.float32)
    n_freq = 16
    freqs = (10000.0 ** (-np.arange(n_freq, dtype=np.float32) / n_freq)).astype(np.float32)
    row = np.repeat(np.arange(NLAT // 64, dtype=np.float32), 64)
    col = np.tile(np.arange(64, dtype=np.float32), NLAT // 64)
    ang = np.concatenate([row[:, None] * freqs, col[:, None] * freqs], axis=-1).astype(np.float32)
    cos = np.ones((NTOK, 32), np.float32)
    sin = np.zeros((NTOK, 32), np.float32)
    cos[:NLAT] = np.cos(ang)
    sin[:NLAT] = np.sin(ang)
    cmat = np.concatenate([U, UT, ident, ones], axis=1)
    return {"c_f32": cmat, "c_maskf": maskf, "c_maskb": maskb,
            "c_cos": cos.reshape(NT, 128, 32).transpose(1, 0, 2).copy(),
            "c_sin": sin.reshape(NT, 128, 32).transpose(1, 0, 2).copy()}


def build(depth=DEPTH, dbg=None):
    P = Prog()
    nc = P.nc
    dt_in = lambda name, shape: nc.dram_tensor(name, list(shape), F32, kind="ExternalInput").ap()
    x_d = dt_in("x", [NLAT, D])
    ctx_d = dt_in("ctx", [NCTX, D])
    c_d = dt_in("c", [D])
    cc_d = dt_in("c_ctx", [D])
    w_mod = dt_in("w_mod", [DEPTH, D, 6 * D])
    b_mod = dt_in("b_mod", [DEPTH, 6 * D])
    n_mix_pre = dt_in("norm_mix_pre", [DEPTH, D])
    n_mix_post = dt_in("norm_mix_post", [DEPTH, D])
    n_ffn_pre = dt_in("norm_ffn_pre", [DEPTH, D])
    n_ffn_post = dt_in("norm_ffn_post", [DEPTH, D])
    w_in = dt_in("w_in", [DEPTH, D, IN_DIM])
    q_norm = dt_in("q_norm", [DEPTH, 64])
    k_norm = dt_in("k_norm", [DEPTH, 64])
    conv_w = dt_in("conv_w", [DEPTH, 5, D])
    conv_b = dt_in("conv_b", [DEPTH, D])
    dtb_f = dt_in("dt_bias_fwd", [DEPTH, 8])
    dtb_b = dt_in("dt_bias_bwd", [DEPTH, 8])
    alog_f = dt_in("a_log_fwd", [DEPTH, 8])
    alog_b = dt_in("a_log_bwd", [DEPTH, 8])
    d_skip = dt_in("d_skip", [DEPTH, 8])
    ssd_norm = dt_in("ssd_norm", [DEPTH, 512])
    w_out = dt_in("w_out", [DEPTH, D, D])
    w_gate = dt_in("w_gate", [DEPTH, D, DFF])
    w_up = dt_in("w_up", [DEPTH, D, DFF])
    w_down = dt_in("w_down", [DEPTH, DFF, D])
    c_f32 = dt_in("c_f32", [128, 512])
    c_maskf = dt_in("c_maskf", [128, 1024])
    c_maskb = dt_in("c_maskb", [128, 1024])
    c_cos = dt_in("c_cos", [128, NT, 32])
    c_sin = dt_in("c_sin", [128, NT, 32])
    out_d = nc.dram_tensor("out", [NLAT, D], F32, kind="ExternalOutput").ap()
    modrow_d = nc.dram_tensor("modrow_s", [DEPTH, 2, 6 * D], F32).ap()
    raw_d = nc.dram_tensor("raw_s", [D, NTOK], BF16).ap()
    sz_d = nc.dram_tensor("sz_s", [NTOK, 512], BF16).ap()
    g_d = nc.dram_tensor("g_s", [NTOK, 512], BF16).ap()
    wsc_d = nc.dram_tensor("wsc_s", [NF, 128, 2048], BF16).ap()
    dbg_d = {}
    if dbg:
        for name, shape in dbg.items():
            if name.startswith("_"):
                continue
            dbg_d[name] = nc.dram_tensor("dbg_" + name, list(shape), F32, kind="ExternalOutput").ap()

    A = Arena(nc, nc.sbuf_base, nc.sbuf_top)
    op, dma = P.op, P.dma
    with contextlib.ExitStack() as st:
        PS = st.enter_context(nc.psum_tensor("ps", [128, 8 * 512], F32))

        def bank(b, n=1):
            return PS[:, b * 512:(b + n) * 512]

        X = A.alloc("X", [128, NT, D], F32)
        CF = A.alloc("CF", [128, 512], F32)
        Uf, UTf, IDf, ONf = CF[:, 0:128], CF[:, 128:256], CF[:, 256:384], CF[:, 384:512]
        IDb = A.alloc("IDb", [128, 128], BF16)
        MKf = A.alloc("MKf", [128, 1024], BF16)
        MKb = A.alloc("MKb", [128, 1024], BF16)
        modT = A.alloc("modT", [128, DEPTH, 48, 2], F32)
        nrmT = A.alloc("nrmT", [128, DEPTH, 2, 8], F32)
        AT = A.alloc("AT", [128, 2, 8], F32)
        ST = A.alloc("ST", [128, 2, 8], F32)
        ssq = A.alloc("ssq", [128, 8], F32)
        small_mark = A.mark()

        dma("sp", lambda e: [e.dma_start(out=CF[:], in_=c_f32)], writes=["CF"])
        dma("pool", lambda e: [e.dma_start(out=MKf[:], in_=c_maskf)], writes=["MKf"])
        dma("pool", lambda e: [e.dma_start(out=MKb[:], in_=c_maskb)], writes=["MKb"])
        op("dve", lambda e: e.tensor_copy(IDb[:], IDf), reads=["CF"], writes=["IDb"])
        for t in range(NT):
            src = x_d[t * 128:(t + 1) * 128, :] if t < 16 else ctx_d[(t - 16) * 128:(t - 15) * 128, :]
            dma("sp", lambda e, t=t, src=src: [e.dma_start(out=X[:, t, :], in_=src)], writes=["X%d" % t])
        for l in range(DEPTH):
            for i, nw in enumerate((n_mix_pre, n_ffn_pre)):
                dma("sp", lambda e, l=l, i=i, nw=nw: [e.dma_start(
                    out=nrmT[:, l, i, :], in_=nw[l].rearrange("(kc p) -> p kc", p=128),
                    allow_slow_non_contiguous=True)], writes=["nrmT"])

        m0 = A.mark()
        stop = (dbg or {}).get("_stop")
        scv = A.alloc("scv", [128, 8, 2], F32)
        wm = [A.alloc("wm%d" % i, [128, 8, 512], F32) for i in range(3)]
        bmt = [A.alloc("bmt%d" % i, [2, 512], F32) for i in range(3)]
        mrow = [A.alloc("mrow%d" % i, [2, 512], F32) for i in range(3)]
        dma("sp", lambda e: [e.dma_start(out=scv[:, :, 0], in_=c_d.rearrange("(kc p) -> p kc", p=128),
                                         allow_slow_non_contiguous=True)], writes=["scv"])
        dma("sp", lambda e: [e.dma_start(out=scv[:, :, 1], in_=cc_d.rearrange("(kc p) -> p kc", p=128),
                                         allow_slow_non_contiguous=True)], writes=["scv"])
        op("act", lambda e: e.activation(scv[:], scv[:], AF.Silu), reads=["scv"], writes=["scv"])
        slabs = [(l, n) for l in range(depth if stop != "init" else 0)
                 for n in range({"s0a": 1, "s0b": 2, "s0c": 3, "s0d": 6, "s0e": 9}.get(stop, 12))]

        def s0_load(i):
            l, n = slabs[i]
            s = i % 3
            dma("sp", lambda e: [e.dma_start(
                out=wm[s][:], in_=w_mod[l][:, n * 512:(n + 1) * 512].rearrange("(kc p) n -> p kc n", p=128))],
                writes=["wm%d" % s])
            dma("sp", lambda e: [e.dma_start(
                out=bmt[s][:], in_=b_mod[l:l + 1, n * 512:(n + 1) * 512].partition_broadcast(2))],
                writes=["bmt%d" % s])

        for i in range(min(2, len(slabs))):
            s0_load(i)
        def s0_mm(i):
            l, n = slabs[i]
            s = i % 3
            bm = 0 if i % 2 == 0 else 2
            for kc in range(8):
                op("pe", lambda e: e.matmul(bank(bm)[0:2, :], scv[:, kc, :], wm[s][:, kc, :],
                                            start=(kc == 0), stop=(kc == 7)),
                   reads=["scv", "wm%d" % s], writes=["psmod%d" % bm])

        def s0_post(i):
            l, n = slabs[i]
            s = i % 3
            bm, bt = (0, 1) if i % 2 == 0 else (2, 3)
            op("dve", lambda e: e.tensor_tensor(mrow[s][:], bank(bm)[0:2, :], bmt[s][:], ALU.add),
               reads=["psmod%d" % bm, "bmt%d" % s], writes=["mrow%d" % s])
            for j in range(4):
                op("pe", lambda e: e.transpose(bank(bt)[:, j * 2:(j + 1) * 2],
                                               mrow[s][0:2, j * 128:(j + 1) * 128], IDf[0:2, 0:2]),
                   reads=["mrow%d" % s, "CF"], writes=["psmt%d" % bt])
            op("act", lambda e: e.copy(
                modT[:, l, n * 4:(n + 1) * 4, :], bank(bt)[:, 0:8].rearrange("p (j t) -> p j t", t=2)),
               reads=["psmt%d" % bt], writes=["modT"])
            dma("act", lambda e: [e.dma_start(out=modrow_d[l, :, n * 512:(n + 1) * 512], in_=mrow[s][:])],
                reads=["mrow%d" % s])

        if slabs:
            s0_mm(0)
        for i in range(len(slabs)):
            if i + 2 < len(slabs):
                s0_load(i + 2)
            if i + 1 < len(slabs):
                s0_mm(i + 1)
            s0_post(i)
        P.barrier()
        A.release(m0)
        stop = (dbg or {}).get("_stop")

        def prep_at(l, i):
            sh0, sc0 = (3 * i) * 8, (3 * i + 1) * 8
            for typ in range(2):
                op("dve", lambda e, typ=typ: e.scalar_tensor_tensor(
                    AT[:, typ, :], modT[:, l, sc0:sc0 + 8, typ], 1.0, nrmT[:, l, i, :], ALU.add, ALU.mult),
                   reads=["modT", "nrmT"], writes=["AT"])
                op("dve", lambda e, typ=typ: e.tensor_copy(ST[:, typ, :], modT[:, l, sh0:sh0 + 8, typ]),
                   reads=["modT"], writes=["ST"])

        def prep_gb(l, i, GB, tmpw):
            g0 = 3 * i + 2
            npost = n_mix_post if i == 0 else n_ffn_post
            dma("sp", lambda e: [e.dma_start(out=tmpw[:], in_=npost[l:l + 1, :].partition_broadcast(128))],
                writes=["tmpw"])
            for typ in range(2):
                dma("sp", lambda e, typ=typ: [e.dma_start(
                    out=GB[:, typ, :], in_=modrow_d[l, typ:typ + 1, g0 * D:(g0 + 1) * D].partition_broadcast(128))],
                    writes=["GB%d" % typ])
                op("dve", lambda e, typ=typ: e.tensor_tensor(GB[:, typ, :], GB[:, typ, :], tmpw[:], ALU.mult),
                   reads=["GB%d" % typ, "tmpw"], writes=["GB%d" % typ])

        def norm_transpose(t, hT, col0, xsbuf, xsname, hname, psb):
            typ = 0 if t < 16 else 1
            xs = xsbuf
            op("act", lambda e: e.activation(xs[:], X[:, t, :], AF.Square, accum_out=ssq[:, 0:1]),
               reads=["X%d" % t], writes=[xsname, "ssq"])
            op("act", lambda e: e.activation(ssq[:, 1:2], ssq[:, 0:1], AF.Sqrt, scale=1.0 / D, bias=EPS),
               reads=["ssq"], writes=["ssq"])
            op("dve", lambda e: e.reciprocal(ssq[:, 2:3], ssq[:, 1:2]), reads=["ssq"], writes=["ssq"])
            op("dve", lambda e: e.tensor_scalar(xs[:], X[:, t, :], ssq[:, 2:3], None, ALU.mult),
               reads=["X%d" % t, "ssq"], writes=[xsname])
            for half in range(2):
                pb = bank(psb + half)
                for k4 in range(4):
                    kc = half * 4 + k4
                    op("pe", lambda e, kc=kc, k4=k4, pb=pb: e.transpose(
                        pb[:, k4 * 128:(k4 + 1) * 128], xs[:, kc * 128:(kc + 1) * 128], IDf),
                       reads=[xsname, "CF"], writes=["psnt%d" % (psb + half)])
                for k4 in range(4):
                    kc = half * 4 + k4
                    if kc % 2 == 0:
                        op("dve", lambda e, kc=kc, k4=k4, pb=pb: e.tensor_scalar(
                            hT[:, kc, col0:col0 + 128], pb[:, k4 * 128:(k4 + 1) * 128],
                            AT[:, typ, kc:kc + 1], ST[:, typ, kc:kc + 1], ALU.mult, ALU.add),
                           reads=["psnt%d" % (psb + half), "AT", "ST"], writes=[hname])
                    else:
                        op("act", lambda e, kc=kc, k4=k4, pb=pb: e.activation(
                            hT[:, kc, col0:col0 + 128], pb[:, k4 * 128:(k4 + 1) * 128], AF.Identity,
                            bias=ST[:, typ, kc:kc + 1], scale=AT[:, typ, kc:kc + 1]),
                           reads=["psnt%d" % (psb + half), "AT", "ST"], writes=[hname])

        def post_residual(t, psb, psname, tmp, tmpname, GB, slot=0):
            typ = 0 if t < 16 else 1
            psn = psname if isinstance(psname, list) else [psname, psname]
            sq = ssq[:, 4 * slot:4 * slot + 4]
            sqn = "ssq" if slot == 0 else "ssq%d" % slot
            for n in range(2):
                op("act", lambda e, n=n: e.activation(tmp[:, n * 512:(n + 1) * 512], bank(psb + n), AF.Square,
                                                      accum_out=sq[:, 3 * n:3 * n + 1]),
                   reads=[psn[n]], writes=[tmpname, sqn])
            op("dve", lambda e: e.tensor_tensor(sq[:, 0:1], sq[:, 0:1], sq[:, 3:4], ALU.add), reads=[sqn], writes=[sqn])
            op("act", lambda e: e.activation(sq[:, 1:2], sq[:, 0:1], AF.Sqrt, scale=1.0 / D, bias=EPS),
               reads=[sqn], writes=[sqn])
            op("dve", lambda e: e.reciprocal(sq[:, 2:3], sq[:, 1:2]), reads=[sqn], writes=[sqn])
            for n in range(2):
                op("dve", lambda e, n=n: e.scalar_tensor_tensor(
                    tmp[:, n * 512:(n + 1) * 512], bank(psb + n), sq[:, 2:3], GB[:, typ, n * 512:(n + 1) * 512],
                    ALU.mult, ALU.mult),
                   reads=[psn[n], sqn, "GB%d" % typ], writes=[tmpname])
            op("pool", lambda e: e.tensor_tensor(X[:, t, :], X[:, t, :], tmp[:], ALU.add),
               reads=["X%d" % t, tmpname], writes=["X%d" % t])

        def dump(name, ap_sb, bufname, rows=128):
            if dbg and name in dbg_d:
                dma("pool", lambda e: [e.dma_start(out=dbg_d[name], in_=ap_sb)], reads=[bufname], dbuf="dump_" + name)

        for l in range(depth if stop not in ("s0", "s0a", "s0b", "s0c", "s0d", "s0e", "init") else 0):
            last = (l == DEPTH - 1)
            ntile = 16 if last else 18
            P.barrier()
            A.release(small_mark)
            prep_at(l, 0)
            m1 = A.mark()
            QT = A.alloc("QT", [128, 4, NTOK], BF16)
            KT2 = A.alloc("KT2", [128, 2, NTOK], BF16)
            VA = A.alloc("VA", [128, NT, 2, 192], BF16)
            dtraw = A.alloc("dtraw", [128, NT, 8], F32)
            m1b = A.mark()
            WIN = A.alloc("WIN", [128, 8, IN_DIM], BF16)
            hTg = [A.alloc("hTg%d" % i, [128, 8, 512], BF16) for i in range(2)]
            xsb = [A.alloc("xsb%d" % i, [128, D], F32) for i in range(1)]
            qn2 = [A.alloc("qn%d" % i, [128, 10, 64], F32) for i in range(2)]
            rt2 = [[A.alloc("rt%d_%d" % (j, i), [128, 10, 32], F32) for i in range(4)] for j in range(2)]
            qkr2 = [A.alloc("qkr%d" % i, [128, 10, 64], F32) for i in range(2)]
            kd2 = [A.alloc("kd%d" % i, [128, 2, 2, 64], F32) for i in range(2)]
            rs2 = [A.alloc("rs%d" % i, [128, 32], F32) for i in range(2)]
            gq = A.alloc("gq", [128, 64], F32)
            gk = A.alloc("gk", [128, 64], F32)
            szt = [A.alloc("szt%d" % i, [128, 512], BF16) for i in range(2)]
            xbt = [A.alloc("xbt%d" % i, [128, 512], BF16) for i in range(2)]
            COS = A.alloc("COS", [128, NT, 32], F32)
            SIN = A.alloc("SIN", [128, NT, 32], F32)
            dma("sp", lambda e: [e.dma_start(out=COS[:], in_=c_cos)], writes=["COS"])
            dma("sp", lambda e: [e.dma_start(out=SIN[:], in_=c_sin)], writes=["SIN"])
            dma("pool", lambda e: [e.dma_start(out=WIN[:, kc, :], in_=w_in[l, kc * 128:(kc + 1) * 128, :])
                                   for kc in range(8)], writes=["WIN"], n_dma=8)
            dma("sp", lambda e: [e.dma_start(out=gq[:], in_=q_norm[l:l + 1, :].partition_broadcast(128))], writes=["gq"])
            dma("sp", lambda e: [e.dma_start(out=gk[:], in_=k_norm[l:l + 1, :].partition_broadcast(128))], writes=["gk"])
            op("pool", lambda e: e.memset(VA[:], 1.0), writes=["VA"])
            groups = [[0, 1, 2, 3], [4, 5, 6, 7], [8, 9, 10, 11], [12, 13, 14, 15], [16, 17]]
            if stop in ("s1a", "s1b", "s1c", "s1b1", "s1b2", "s1b3"):
                groups = groups[:1]
            lim = {"s1b1": 1, "s1b2": 2, "s1b3": 3}.get(stop, 9)
            for ti, t in enumerate(groups[0]):
                norm_transpose(t, hTg[0], ti * 128, xsb[0], "xsb0", "hTg0", 0)
            for gi, tiles in enumerate(groups):
                hs = gi % 2
                hT = hTg[hs]
                ntk = len(tiles) * 128
                tok0 = tiles[0] * 128
                nxt_tiles = groups[gi + 1] if gi + 1 < len(groups) else []
                def tile_mm(ti, t):
                    cs = slice(ti * 128, (ti + 1) * 128)
                    st_ = t % 2
                    bq, bkv, bz = (2, 3, 4) if st_ == 0 else (5, 6, 7)
                    nq_, nkv_, nz_ = "pb%d" % bq, "pb%d" % bkv, "pb%d" % bz
                    qn, rt, qkr, kd, rs = qn2[st_], rt2[st_], qkr2[st_], kd2[st_], rs2[st_]
                    qnn, qkrn, kdn, rsn = "qn%d" % st_, "qkr%d" % st_, "kd%d" % st_, "rs%d" % st_
                    rtn = ["rt%d_%d" % (st_, i) for i in range(4)]
                    for kc in range(8):
                        f, la = (kc == 0), (kc == 7)
                        rd = ["hTg%d" % hs, "WIN"]
                        op("pe", lambda e: e.matmul(bank(bq), hT[:, kc, cs], WIN[:, kc, 0:512], start=f, stop=la),
                           reads=rd, writes=[nq_])
                        op("pe", lambda e: e.matmul(bank(bkv)[:, 0:256], hT[:, kc, cs], WIN[:, kc, 512:768], start=f, stop=la),
                           reads=rd, writes=[nkv_])
                        op("pe", lambda e: e.matmul(bank(bz), hT[:, kc, cs], WIN[:, kc, 768:1280], start=f, stop=la),
                           reads=rd, writes=[nz_])
                    for kc in range(8):
                        op("pe", lambda e: e.matmul(bank(bkv)[:, 256:264], hT[:, kc, cs], WIN[:, kc, 2304:2312],
                                                    start=(kc == 0), stop=(kc == 7)),
                           reads=["hTg%d" % hs, "WIN"], writes=[nkv_])

                def tile_s1(ti, t):
                    cs = slice(ti * 128, (ti + 1) * 128)
                    st_ = t % 2
                    bq, bkv, bz = (2, 3, 4) if st_ == 0 else (5, 6, 7)
                    nq_, nkv_, nz_ = "pb%d" % bq, "pb%d" % bkv, "pb%d" % bz
                    qn, rt, qkr, kd, rs = qn2[st_], rt2[st_], qkr2[st_], kd2[st_], rs2[st_]
                    qnn, qkrn, kdn, rsn = "qn%d" % st_, "qkr%d" % st_, "kd%d" % st_, "rs%d" % st_
                    rtn = ["rt%d_%d" % (st_, i) for i in range(4)]
                    sqv = qkr[:].rearrange("p h d -> p (h d)")
                    op("act", lambda e: e.activation(sqv[:, 0:512], bank(bq), AF.Square), reads=[nq_], writes=[qkrn])
                    op("act", lambda e: e.activation(sqv[:, 512:640], bank(bkv)[:, 0:128], AF.Square), reads=[nkv_], writes=[qkrn])
                    op("dve", lambda e: e.tensor_reduce(rs[:, 0:10], qkr[:], AX.X, ALU.add), reads=[qkrn], writes=[rsn])
                    op("act", lambda e: e.activation(rs[:, 10:20], rs[:, 0:10], AF.Sqrt, scale=1.0 / 64, bias=EPS),
                       reads=[rsn], writes=[rsn])
                    op("dve", lambda e: e.reciprocal(rs[:, 20:30], rs[:, 10:20]), reads=[rsn], writes=[rsn])
                    op("dve", lambda e: e.tensor_tensor(
                        qn[:, 0:8, :], bank(bq).rearrange("p (h d) -> p h d", d=64),
                        rs[:, 20:28].unsqueeze(2).to_broadcast([128, 8, 64]), ALU.mult),
                       reads=[nq_, rsn], writes=[qnn])
                    op("dve", lambda e: e.tensor_tensor(
                        qn[:, 8:10, :], bank(bkv)[:, 0:128].rearrange("p (h d) -> p h d", d=64),
                        rs[:, 28:30].unsqueeze(2).to_broadcast([128, 2, 64]), ALU.mult),
                       reads=[nkv_, rsn], writes=[qnn])
                    for g in range(2):
                        op("act", lambda e: e.copy(VA[:, t, g, 64:128], bank(bkv)[:, 128 + g * 64:192 + g * 64]),
                           reads=[nkv_], writes=["VA"])
                    op("act", lambda e: e.activation(szt[st_][:], bank(bz), AF.Silu), reads=[nz_], writes=["szt%d" % st_])
                    dma("sp", lambda e: [e.dma_start(out=sz_d[t * 128:(t + 1) * 128, :], in_=szt[st_][:])], reads=["szt%d" % st_])
                    op("dve", lambda e: e.tensor_copy(dtraw[:, t, :], bank(bkv)[:, 256:264]), reads=[nkv_], writes=["dtraw"])
                    op("pool", lambda e: e.tensor_tensor(qn[:, 0:8, :], qn[:, 0:8, :],
                                                         gq[:].unsqueeze(1).to_broadcast([128, 8, 64]), ALU.mult),
                       reads=[qnn, "gq"], writes=[qnn])
                    op("pool", lambda e: e.tensor_tensor(qn[:, 8:10, :], qn[:, 8:10, :],
                                                         gk[:].unsqueeze(1).to_broadcast([128, 2, 64]), ALU.mult),
                       reads=[qnn, "gk"], writes=[qnn])
                    cosb = COS[:, t, :].unsqueeze(1).to_broadcast([128, 10, 32])
                    sinb = SIN[:, t, :].unsqueeze(1).to_broadcast([128, 10, 32])
                    x1, x2 = qn[:, :, 0:32], qn[:, :, 32:64]
                    op("dve", lambda e: e.tensor_tensor(rt[0][:], x1, cosb, ALU.mult), reads=[qnn, "COS"], writes=[rtn[0]])
                    op("pool", lambda e: e.tensor_tensor(rt[1][:], x2, sinb, ALU.mult), reads=[qnn, "SIN"], writes=[rtn[1]])
                    op("dve", lambda e: e.tensor_tensor(qkr[:, :, 0:32], rt[0][:], rt[1][:], ALU.subtract),
                       reads=[rtn[0], rtn[1]], writes=[qkrn])
                    op("pool", lambda e: e.tensor_tensor(rt[2][:], x1, sinb, ALU.mult), reads=[qnn, "SIN"], writes=[rtn[2]])
                    op("dve", lambda e: e.tensor_tensor(rt[3][:], x2, cosb, ALU.mult), reads=[qnn, "COS"], writes=[rtn[3]])
                    op("dve", lambda e: e.tensor_tensor(qkr[:, :, 32:64], rt[2][:], rt[3][:], ALU.add),
                       reads=[rtn[2], rtn[3]], writes=[qkrn])

                def tile_s2(ti, t):
                    cs = slice(ti * 128, (ti + 1) * 128)
                    st_ = t % 2
                    bq, bkv, bz = (2, 3, 4) if st_ == 0 else (5, 6, 7)
                    nq_, nkv_, nz_ = "pb%d" % bq, "pb%d" % bkv, "pb%d" % bz
                    qn, rt, qkr, kd, rs = qn2[st_], rt2[st_], qkr2[st_], kd2[st_], rs2[st_]
                    qnn, qkrn, kdn, rsn = "qn%d" % st_, "qkr%d" % st_, "kd%d" % st_, "rs%d" % st_
                    rtn = ["rt%d_%d" % (st_, i) for i in range(4)]
                    for r in range(2):
                        op("dve", lambda e: e.tensor_copy(kd[:, :, r, :], qkr[:, 8:10, :]), reads=[qkrn], writes=[kdn])
                    for pr in range(4):
                        op("pe", lambda e: e.transpose(bank(0)[:, pr * 128:(pr + 1) * 128],
                                                       qkr[:, 2 * pr:2 * pr + 2, :].rearrange("p h d -> p (h d)"), IDf),
                           reads=[qkrn, "CF"], writes=["psnt0"])
                    for g in range(2):
                        op("pe", lambda e: e.transpose(bank(1)[:, g * 128:(g + 1) * 128],
                                                       kd[:, g].rearrange("p r d -> p (r d)"), IDf),
                           reads=[kdn, "CF"], writes=["psnt1"])
                    for pr in range(4):
                        op("act", lambda e: e.copy(QT[:, pr, t * 128:(t + 1) * 128], bank(0)[:, pr * 128:(pr + 1) * 128]),
                           reads=["psnt0"], writes=["QT"])
                    for g in range(2):
                        op("dve", lambda e: e.tensor_copy(KT2[:, g, t * 128:(t + 1) * 128], bank(1)[:, g * 128:(g + 1) * 128]),
                           reads=["psnt1"], writes=["KT2"])
                    if ti < len(nxt_tiles):
                        norm_transpose(nxt_tiles[ti], hTg[1 - hs], ti * 128, xsb[0], "xsb0", "hTg%d" % (1 - hs), 0)

                tl_ = tiles if stop != "s1a" else []
                if tl_:
                    tile_mm(0, tl_[0])
                    tile_s1(0, tl_[0])
                for ti, t in enumerate(tl_):
                    if ti + 1 < len(tl_):
                        tile_mm(ti + 1, tl_[ti + 1])
                        tile_s1(ti + 1, tl_[ti + 1])
                    tile_s2(ti, t)
                P.mute = False
                for ti in range(len(tiles), len(nxt_tiles)):
                    norm_transpose(nxt_tiles[ti], hTg[1 - hs], ti * 128, xsb[0], "xsb0", "hTg%d" % (1 - hs), 0)
                for fc in range(8 if stop not in ("s1a", "s1b", "s1b1", "s1b2", "s1b3") else 0):
                    pb = 5 + (fc % 2)
                    pbn = "pb%d" % pb
                    for kc in range(8):
                        op("pe", lambda e, kc=kc, fc=fc, pb=pb: e.matmul(
                            bank(pb)[:, 0:ntk], WIN[:, kc, 1280 + fc * 128:1280 + (fc + 1) * 128], hT[:, kc, 0:ntk],
                            start=(kc == 0), stop=(kc == 7)),
                           reads=["hTg%d" % hs, "WIN"], writes=[pbn])
                    xs_ = fc % 2
                    if fc % 2 == 0:
                        op("dve", lambda e, pb=pb, xs_=xs_: e.tensor_copy(xbt[xs_][:, 0:ntk], bank(pb)[:, 0:ntk]),
                           reads=[pbn], writes=["xbt%d" % xs_])
                    else:
                        op("act", lambda e, pb=pb, xs_=xs_: e.copy(xbt[xs_][:, 0:ntk], bank(pb)[:, 0:ntk]),
                           reads=[pbn], writes=["xbt%d" % xs_])
                    dma("sp", lambda e, fc=fc, xs_=xs_: [e.dma_start(
                        out=raw_d[fc * 128:(fc + 1) * 128, tok0:tok0 + ntk], in_=xbt[xs_][:, 0:ntk])],
                        reads=["xbt%d" % xs_])
            if dbg:
                dump("QT", QT[:, 0, 0:512], "QT")
            P.barrier()
            A.release(m1b)
            if stop in ("s1", "s1a", "s1b", "s1c", "s1b1", "s1b2", "s1b3"):
                break

            ATT = A.alloc("ATT", [128, 4, NTOK], BF16)
            m3 = A.mark()
            PT = [A.alloc("PT%d" % i, [128, 1024], BF16) for i in range(3)]
            rec = [A.alloc("rec%d" % i, [128, 1024], F32) for i in range(1)]
            ocp = [A.alloc("ocp%d" % i, [128, 1024], F32) for i in range(2)]
            KTZ = A.alloc("KTZ", [128, 2, 2, NTOK], BF16)
            op("pool", lambda e: e.memset(KTZ[:], 0.0), writes=["KTZ"])
            for g in range(2):
                for par in range(2):
                    rows = slice(par * 64, par * 64 + 64)
                    op("dve" if par == 0 else "act",
                       (lambda e: e.tensor_copy(KTZ[rows, g, par, :], KT2[rows, g, :])) if par == 0 else
                       (lambda e: e.copy(KTZ[rows, g, par, :], KT2[rows, g, :])),
                       reads=["KT2", "KTZ"], writes=["KTZ"])
            for f in range(NF):
                wv = wsc_d[f].rearrange("p (kc w n) -> p kc w n", kc=8, w=2)
                dma("pool", lambda e: [
                    e.dma_start(out=wv[:, :, 0, :], in_=w_gate[l][:, f * 128:(f + 1) * 128].rearrange("(kc p) n -> p kc n", p=128)),
                    e.dma_start(out=wv[:, :, 1, :], in_=w_up[l][:, f * 128:(f + 1) * 128].rearrange("(kc p) n -> p kc n", p=128))],
                    writes=["wscd%d" % f], n_dma=2, dbuf="wscpre%d" % (f % 4))
            qsets = [(0, 1024, list(range(NT))), (1024, 1024, list(range(NT)))]
            if not last:
                qsets.append((NLAT, 256, [16, 17]))
            seq = []
            gi_ = 0
            for h in range(8):
                for (q0, nq, kts) in qsets:
                    for ki, kt in enumerate(kts):
                        seq.append((h, q0, nq, ki, kt, len(kts), gi_))
                    gi_ += 1

            def emit_s(i):
                h, q0, nq, ki, kt, nk, gi = seq[i]
                g, pr, par = h // 4, h // 2, h % 2
                sl = i % 3
                for j in range((nq + 511) // 512):
                    w = min(512, nq - j * 512)
                    op("pe", lambda e: e.matmul(
                        bank(2 * sl + j)[:, 0:w], KTZ[:, g, par, kt * 128:(kt + 1) * 128],
                        QT[:, pr, q0 + j * 512:q0 + j * 512 + w], start=True, stop=True),
                       reads=["KTZ", "QT"], writes=["pss%d" % sl])

            def emit_rest(i):
                h, q0, nq, ki, kt, nk, gi = seq[i]
                g, pr, par = h // 4, h // 2, h % 2
                orow = slice(par * 64, par * 64 + 64)
                srow = slice((1 - par) * 64, (1 - par) * 64 + 64)
                vcols = slice(64, 192) if par == 0 else slice(0, 128)
                sl = i % 3
                nch = (nq + 511) // 512
                op("act", lambda e: e.activation(PT[sl][:, 0:nq], bank(2 * sl, 2)[:, 0:nq], AF.Exp, scale=0.125),
                   reads=["pss%d" % sl], writes=["PT%d" % sl])
                for j in range(nch):
                    w = min(512, nq - j * 512)
                    op("pe", lambda e: e.matmul(
                        bank(6 + j)[:, 0:w], VA[:, kt, g, vcols], PT[sl][:, j * 512:j * 512 + w],
                        start=(ki == 0), stop=(ki == nk - 1)),
                       reads=["VA", "PT%d" % sl], writes=["pso"])
                if ki == nk - 1:
                    oc = ocp[gi % 2]
                    ocn = "ocp%d" % (gi % 2)
                    for j in range(nch):
                        w = min(512, nq - j * 512)
                        op("dve", lambda e: e.tensor_copy(oc[:, j * 512:j * 512 + w], bank(6 + j)[:, 0:w]),
                           reads=["pso"], writes=[ocn])
                    op("dve", lambda e: e.reciprocal(rec[0][orow, 0:nq], oc[srow, 0:nq]), reads=[ocn], writes=["rec0"])
                    op("dve", lambda e: e.tensor_tensor(ATT[orow, pr, q0:q0 + nq], oc[orow, 0:nq], rec[0][orow, 0:nq], ALU.mult),
                       reads=[ocn, "rec0"], writes=["ATT"])

            emit_s(0)
            emit_s(1)
            for i in range(len(seq)):
                if i + 2 < len(seq):
                    emit_s(i + 2)
                emit_rest(i)
            if dbg:
                dump("ATT", ATT[:, 0, 0:512], "ATT")
            P.barrier()
            A.release(m3)
            if dbg and dbg.get("_stop") == "attn":
                break

            m_att = A.mark()
            A.release(m1)
            XBT = A.alloc("XBT", [128, 8, NTOK], BF16)
            assert A.mark() <= m1 + 18432 + 9216 + 13824
            A.release(m1)
            HBI = A.alloc("HBI", [128, NT, 512], BF16)
            A.release(m_att)
            XST = A.alloc("XST", [128, NT, 512], BF16)
            BTK = A.alloc("BTK", [128, NT, 256], BF16)
            m2 = A.mark()
            rawb = [A.alloc("rawb%d" % i, [128, NTOK], BF16) for i in range(2)]
            accs = [A.alloc("acc%d" % i, [128, NTOK], F32) for i in range(2)]
            cw = A.alloc("cw", [128, 8, 5], F32)
            cb = A.alloc("cb", [128, 8], F32)
            dma("sp", lambda e: [e.dma_start(out=cw[:, :, k_], in_=conv_w[l, k_].rearrange("(fc p) -> p fc", p=128),
                                             allow_slow_non_contiguous=True) for k_ in range(5)],
                writes=["cw"], n_dma=5)
            dma("sp", lambda e: [e.dma_start(out=cb[:], in_=conv_b[l].rearrange("(fc p) -> p fc", p=128),
                                             allow_slow_non_contiguous=True)], writes=["cb"])
            def tok_major(t, which):
                tcs = slice(t * 128, (t + 1) * 128)
                if which == 0:
                    pbx = 2 * (t % 2)
                    for k4 in range(4):
                        op("pe", lambda e: e.matmul(bank(pbx)[:, k4 * 128:(k4 + 1) * 128], XBT[:, k4, tcs], IDb[:],
                                                    start=True, stop=True),
                           reads=["XBT%d" % k4, "IDb"], writes=["pb%d" % pbx])
                    op("act", lambda e: e.copy(XST[:, t, :], bank(pbx)), reads=["pb%d" % pbx], writes=["XST"])
                else:
                    pbb = 2 * (t % 2) + 1
                    for g in range(2):
                        op("pe", lambda e: e.matmul(bank(pbb)[:, g * 128:(g + 1) * 128], XBT[:, 4 + g, tcs], IDb[:],
                                                    start=True, stop=True),
                           reads=["XBT%d" % (4 + g), "IDb"], writes=["pb%d" % pbb])
                    op("act", lambda e: e.copy(BTK[:, t, :], bank(pbb)[:, 0:256]), reads=["pb%d" % pbb], writes=["BTK"])

            for fc in range(8):
                s = fc % 2
                rb = rawb[s]
                rbn = "rawb%d" % s
                acc = accs[s]
                an = "acc%d" % s
                dma("sp", lambda e: [e.dma_start(out=rb[:], in_=raw_d[fc * 128:(fc + 1) * 128, :])], writes=[rbn])
                for (a0, b0) in ((0, NLAT), (NLAT, NTOK)):
                    op("dve", lambda e: e.tensor_scalar(acc[:, a0:b0], rb[:, a0:b0], cw[:, fc, 2:3], None, ALU.mult),
                       reads=[rbn, "cw"], writes=[an])
                    for (kk, do, si) in ((0, 2, 0), (1, 1, 0), (3, 0, 1), (4, 0, 2)):
                        n_ = (b0 - a0) - max(do, si)
                        op("dve", lambda e: e.scalar_tensor_tensor(
                            acc[:, a0 + do:a0 + do + n_], rb[:, a0 + si:a0 + si + n_], cw[:, fc, kk:kk + 1],
                            acc[:, a0 + do:a0 + do + n_], ALU.mult, ALU.add),
                           reads=[rbn, "cw", an], writes=[an])
                op("act", lambda e: e.activation(XBT[:, fc, :], acc[:], AF.Silu, bias=cb[:, fc:fc + 1]),
                   reads=[an, "cb"], writes=["XBT%d" % fc])
                if fc == 4:
                    for t in range(NT):
                        tok_major(t, 0)
                if fc == 7:
                    for t in range(NT):
                        tok_major(t, 1)
            if dbg:
                dump("XBT", XBT[:, 0, 0:512], "XBT0")
            P.barrier()
            A.release(m2)
            if stop == "conv":
                break

            sma = lambda nm: A.alloc(nm, [128, NT, 8], F32)
            a_d = [sma("a_f"), sma("a_b")]
            w_d = [sma("w_f"), sma("w_b")]
            ea_d = [sma("ea_f"), sma("ea_b")]
            cd_d = [sma("cd_f"), sma("cd_b")]
            de_d = [sma("de_f"), sma("de_b")]
            tmpa = sma("tmpa")
            prm = A.alloc("prm", [128, 5, 8], F32)
            nal = A.alloc("nal", [128, 2, 8], F32)
            ssdw = A.alloc("ssdw", [128, 512], F32)
            aU = [A.alloc("aU%d" % i, [128, 1024], F32) for i in range(2)]
            Ed = [A.alloc("E%d" % i, [128, 1024], BF16) for i in range(2)]
            MT = A.alloc("MT", [128, 1024], BF16)
            GTs = A.alloc("GTs", [128, 256], F32)
            t1 = A.alloc("t1", [128, 512], F32)
            t2 = A.alloc("t2", [128, 512], F32)
            ys = A.alloc("ys", [128, 512], F32)
            gout = [A.alloc("gout%d" % i, [128, 512], BF16) for i in range(1)] * 2
            xw = A.alloc("xw", [128, 512], BF16)
            xw_b = A.alloc("xw_b", [128, 512], BF16)
            Hs = A.alloc("Hs", [128, 512], F32)
            Hfb = A.alloc("Hfb", [128, 512], BF16)
            szc = [A.alloc("szc%d" % i, [128, 512], BF16) for i in range(1)] * 2
            P.region = True
            P.limit = (dbg or {}).get("_lim", 10 ** 9)
            for i, src in enumerate((dtb_f, dtb_b, alog_f, alog_b, d_skip)):
                dma("sp", lambda e: [e.dma_start(out=prm[:, i, :], in_=src[l:l + 1, :].partition_broadcast(128))],
                    writes=["prm%d" % i])
            dma("sp", lambda e: [e.dma_start(out=ssdw[:], in_=ssd_norm[l:l + 1, :].partition_broadcast(128))],
                writes=["ssdw"])
            bc38 = lambda ap: ap.unsqueeze(1).to_broadcast([128, NT, 8])
            flat = lambda tns: tns[:].rearrange("p c h -> p (c h)")
            for d in range(2):
                dn = "fb"[d]
                op("dve", lambda e: e.tensor_tensor(tmpa[:], dtraw[:], bc38(prm[:, d, :]), ALU.add),
                   reads=["dtraw", "prm%d" % d], writes=["tmpa"])
                op("act", lambda e: e.activation(tmpa[:], tmpa[:], AF.Exp), reads=["tmpa"], writes=["tmpa"])
                op("act", lambda e: e.activation(tmpa[:], tmpa[:], AF.Ln, bias=1.0), reads=["tmpa"], writes=["tmpa"])
                op("act", lambda e: e.activation(w_d[d][:], tmpa[:], AF.Ln), reads=["tmpa"], writes=["w_" + dn])
                op("act", lambda e: e.activation(nal[:, d, :], prm[:, 2 + d, :], AF.Exp), reads=["prm%d" % (2 + d)], writes=["nal"])
                op("dve", lambda e: e.scalar_tensor_tensor(a_d[d][:], tmpa[:], -1.0, bc38(nal[:, d, :]), ALU.mult, ALU.mult),
                   reads=["tmpa", "nal"], writes=["a_" + dn])
                op("pe", lambda e: e.matmul(bank(d)[:, 0:NT * 8], Uf if d == 0 else UTf, flat(a_d[d]), start=True, stop=True),
                   reads=["a_" + dn, "CF"], writes=["pb%d" % d])
                op("pe", lambda e: e.matmul(bank(2 + d)[:, 0:NT * 8], ONf, flat(a_d[d]), start=True, stop=True),
                   reads=["a_" + dn, "CF"], writes=["pb%d" % (2 + d)])
                op("dve", lambda e: e.tensor_tensor(flat(w_d[d]), flat(w_d[d]), bank(d)[:, 0:NT * 8], ALU.subtract),
                   reads=["w_" + dn, "pb%d" % d], writes=["w_" + dn])
                op("act", lambda e: e.activation(flat(ea_d[d]), bank(d)[:, 0:NT * 8], AF.Exp), reads=["pb%d" % d], writes=["ea_" + dn])
                op("act", lambda e: e.activation(flat(cd_d[d]), bank(2 + d)[:, 0:NT * 8], AF.Exp), reads=["pb%d" % (2 + d)], writes=["cd_" + dn])
                op("dve", lambda e: e.tensor_tensor(flat(tmpa), flat(w_d[d]), bank(2 + d)[:, 0:NT * 8], ALU.add),
                   reads=["w_" + dn, "pb%d" % (2 + d)], writes=["tmpa"])
                op("act", lambda e: e.activation(de_d[d][:], tmpa[:], AF.Exp), reads=["tmpa"], writes=["de_" + dn])
            P.region = False
            if stop == "ssd0":
                P.mute = True
            h3 = lambda ap: ap.rearrange("p (h q) -> p h q", q=64)
            bh = lambda ap: ap.unsqueeze(2).to_broadcast([128, 8, 64])

            def state_update(c, d, hbuf, hname, pbs=0):
                dn = "fb"[d]
                op("dve", lambda e: e.tensor_tensor(h3(xw[:]), h3(XST[:, c, :]), bh(de_d[d][:, c, :]), ALU.mult),
                   reads=["XST", "de_" + dn], writes=["xw"])
                for g in range(2):
                    op("pe", lambda e: e.matmul(bank(pbs)[:, g * 256:(g + 1) * 256], BTK[:, c, g * 128:(g + 1) * 128],
                                                xw[:, g * 256:(g + 1) * 256], start=True, stop=True),
                       reads=["BTK", "xw"], writes=["pb%d" % pbs])
                op("dve", lambda e: e.tensor_tensor(h3(hbuf[:]), h3(hbuf[:]), bh(cd_d[d][:, c, :]), ALU.mult),
                   reads=[hname, "cd_" + dn], writes=[hname])
                op("dve", lambda e: e.tensor_tensor(hbuf[:], hbuf[:], bank(pbs), ALU.add),
                   reads=[hname, "pb%d" % pbs], writes=[hname])

            op("pool", lambda e: e.memset(Hs[:], 0.0), writes=["Hs"])
            order1 = [17, 16] + list(range(15, -1, -1))

            def p1_mm(i):
                c = order1[i]
                xb, xbn, pbs = (xw, "xw", 0) if i % 2 == 0 else (xw_b, "xw_b", 7)
                op("dve", lambda e: e.tensor_tensor(h3(xb[:]), h3(XST[:, c, :]), bh(de_d[1][:, c, :]), ALU.mult),
                   reads=["XST", "de_b"], writes=[xbn])
                for g in range(2):
                    op("pe", lambda e: e.matmul(bank(pbs)[:, g * 256:(g + 1) * 256], BTK[:, c, g * 128:(g + 1) * 128],
                                                xb[:, g * 256:(g + 1) * 256], start=True, stop=True),
                       reads=["BTK", xbn], writes=["pb%d" % pbs])

            def p1_upd(i):
                c = order1[i]
                pbs = 0 if i % 2 == 0 else 7
                op("act", lambda e: e.copy(HBI[:, c, :], Hs[:]), reads=["Hs"], writes=["HBI"])
                op("dve", lambda e: e.tensor_tensor(h3(Hs[:]), h3(Hs[:]), bh(cd_d[1][:, c, :]), ALU.mult),
                   reads=["Hs", "cd_b"], writes=["Hs"])
                op("dve", lambda e: e.tensor_tensor(Hs[:], Hs[:], bank(pbs), ALU.add),
                   reads=["Hs", "pb%d" % pbs], writes=["Hs"])

            p1_mm(0)
            for i in range(NT):
                if i + 1 < NT:
                    p1_mm(i + 1)
                p1_upd(i)
            if stop == "ssd1":
                P.mute = True
            op("pool", lambda e: e.memset(Hs[:], 0.0), reads=["Hs"], writes=["Hs"])
            op("pool", lambda e: e.memset(Hfb[:], 0.0), writes=["Hfb"])
            order2 = [16, 17] + list(range(16))

            def part_a1(ci):
                c = order2[ci]
                tcs = slice(c * 128, (c + 1) * 128)
                for g in range(2):
                    op("pe", lambda e: e.matmul(bank(0)[:, g * 128:(g + 1) * 128], XBT[:, 4 + g, tcs], XBT[:, 6 + g, tcs],
                                                start=True, stop=True),
                       reads=["XBT"], writes=["pb0"])
                op("act", lambda e: e.copy(GTs[:], bank(0)[:, 0:256]), reads=["pb0"], writes=["GTs"])
                for d in range(2):
                    dn = "fb"[d]
                    Um = Uf if d == 0 else UTf
                    MK = MKf if d == 0 else MKb
                    op("pool", lambda e: e.tensor_tensor(
                        aU[d][:].rearrange("p (h i) -> p h i", i=128), Um.unsqueeze(1).to_broadcast([128, 8, 128]),
                        a_d[d][:, c, :].unsqueeze(2).to_broadcast([128, 8, 128]), ALU.mult),
                       reads=["CF", "a_" + dn], writes=["aU%d" % d])
                    for hf in range(2):
                        pbn = 1 + 2 * d + hf
                        op("pe", lambda e: e.matmul(bank(pbn), ONf, aU[d][:, hf * 512:(hf + 1) * 512], start=True, stop=False),
                           reads=["CF", "aU%d" % d], writes=["pb%d" % pbn])
                        op("pe", lambda e: e.matmul(bank(pbn), IDb[:], MK[:, hf * 512:(hf + 1) * 512], start=False, stop=True),
                           reads=["IDb", "MK" + dn], writes=["pb%d" % pbn])
                    for h in range(8):
                        pbn = 1 + 2 * d + h // 4
                        op("act", lambda e: e.activation(Ed[d][:, h * 128:(h + 1) * 128],
                                                         bank(pbn)[:, (h % 4) * 128:(h % 4 + 1) * 128], AF.Exp,
                                                         bias=w_d[d][:, c, h:h + 1]),
                           reads=["pb%d" % pbn, "w_" + dn], writes=["E%d" % d])

            def part_a2(ci):
                c = order2[ci]
                py = 7
                op("pool", lambda e: e.tensor_tensor(Ed[0][:], Ed[0][:], Ed[1][:], ALU.add), reads=["E0", "E1"], writes=["E0"])
                for g in range(2):
                    op("dve", lambda e: e.tensor_tensor(
                        MT[:, g * 512:(g + 1) * 512].rearrange("p (h i) -> p h i", i=128),
                        Ed[0][:, g * 512:(g + 1) * 512].rearrange("p (h i) -> p h i", i=128),
                        GTs[:, g * 128:(g + 1) * 128].unsqueeze(1).to_broadcast([128, 4, 128]), ALU.mult),
                       reads=["E0", "GTs"], writes=["MT"])
                for h in range(8):
                    op("pe", lambda e: e.matmul(bank(py)[:, h * 64:(h + 1) * 64], MT[:, h * 128:(h + 1) * 128],
                                                XST[:, c, h * 64:(h + 1) * 64], start=True, stop=True),
                       reads=["MT", "XST"], writes=["pb%d" % py])

            def part_b1(ci):
                c = order2[ci]
                tcs = slice(c * 128, (c + 1) * 128)
                py = 7
                zs = ci % 2
                dma("sp", lambda e: [e.dma_start(out=szc[0][:], in_=sz_d[c * 128:(c + 1) * 128, :])], writes=["szc0"])
                op("pool", lambda e: e.tensor_tensor(h3(ys[:]), h3(XST[:, c, :]), bh(prm[:, 4, :]), ALU.mult),
                   reads=["XST", "prm4", "ys"], writes=["ys"])
                for g in range(2):
                    op("pe", lambda e: e.matmul(bank(5)[:, g * 256:(g + 1) * 256], XBT[:, 6 + g, tcs],
                                                Hfb[:, g * 256:(g + 1) * 256], start=True, stop=True),
                       reads=["XBT", "Hfb"], writes=["pb5"])
                    op("pe", lambda e: e.matmul(bank(6)[:, g * 256:(g + 1) * 256], XBT[:, 6 + g, tcs],
                                                HBI[:, c, g * 256:(g + 1) * 256], start=True, stop=True),
                       reads=["XBT", "HBI"], writes=["pb6"])
                op("dve", lambda e: e.tensor_tensor(h3(t1[:]), h3(bank(5)), bh(ea_d[0][:, c, :]), ALU.mult),
                   reads=["pb5", "ea_f"], writes=["t1"])
                op("dve", lambda e: e.tensor_tensor(h3(t2[:]), h3(bank(6)), bh(ea_d[1][:, c, :]), ALU.mult),
                   reads=["pb6", "ea_b"], writes=["t2"])
                op("dve", lambda e: e.tensor_tensor(t1[:], t1[:], t2[:], ALU.add), reads=["t1", "t2"], writes=["t1"])
                op("dve", lambda e: e.tensor_tensor(t1[:], t1[:], ys[:], ALU.add), reads=["t1", "ys"], writes=["t1"])
                op("dve", lambda e: e.tensor_tensor(ys[:], bank(py), t1[:], ALU.add), reads=["pb%d" % py, "t1", "ys"], writes=["ys"])
                op("dve", lambda e: e.tensor_tensor(ys[:], ys[:], szc[0][:], ALU.mult), reads=["ys", "szc0"], writes=["ys"])
                op("dve", lambda e: e.scalar_tensor_tensor(t2[:], ys[:], 1.0, ys[:], ALU.mult, ALU.mult, accum_out=ssq[:, 0:1]),
                   reads=["ys", "t2"], writes=["t2", "ssq"])
                op("act", lambda e: e.activation(ssq[:, 1:2], ssq[:, 0:1], AF.Ln, scale=1.0 / 512, bias=EPS),
                   reads=["ssq"], writes=["ssq"])
                op("act", lambda e: e.activation(ssq[:, 2:3], ssq[:, 1:2], AF.Exp, scale=-0.5),
                   reads=["ssq"], writes=["ssq"])
                op("dve", lambda e: e.scalar_tensor_tensor(gout[zs][:], ys[:], ssq[:, 2:3], ssdw[:], ALU.mult, ALU.mult),
                   reads=["ys", "ssq", "ssdw"], writes=["gout0"])
                dma("sp", lambda e: [e.dma_start(out=g_d[c * 128:(c + 1) * 128, :], in_=gout[zs][:])], reads=["gout0"])
                if dbg and c == 0:
                    dump("G", gout[zs][:], "gout0")

            def part_b2(ci):
                c = order2[ci]
                state_update(c, 0, Hs, "Hs", 0)
                op("act", lambda e: e.copy(Hfb[:], Hs[:]), reads=["Hs"], writes=["Hfb"])

            first = 2 if last else 0
            for ci in range(first):
                part_b2(ci)
            part_a1(first)
            part_a2(first)
            for ci in range(first, NT):
                if ci + 1 < NT:
                    part_a1(ci + 1)
                part_b1(ci)
                if ci + 1 < NT:
                    part_a2(ci + 1)
                part_b2(ci)
            P.mute = False
            P.barrier()
            A.release(m2)
            if stop in ("ssd", "ssd0", "ssd1", "ssd2a", "ssd2b", "ssd2c"):
                break

            A.release(m_att)
            GB = A.alloc("GB", [128, 2, D], F32)
            tmpo = A.alloc("tmpo", [128, D], F32)
            prep_gb(l, 0, GB, tmpo)
            WO = A.alloc("WO", [128, 8, D], BF16)
            tmpo2 = A.alloc("tmpo2", [128, D], F32)
            gtk = [A.alloc("gtk%d" % i, [128, 512], BF16) for i in range(2)]
            gT = [A.alloc("gT%d" % i, [128, 512], BF16) for i in range(2)]
            dma("pool", lambda e: [e.dma_start(out=WO[:], in_=w_out[l].rearrange("(kc p) n -> p kc n", p=128))],
                writes=["WO"])
            def op_tr(t):
                s = t % 2
                dma("sp", lambda e: [e.dma_start(out=gtk[s][:], in_=g_d[t * 128:(t + 1) * 128, :])], writes=["gtk%d" % s])
                for k4 in range(4):
                    op("pe", lambda e: e.matmul(bank(2 + s)[:, k4 * 128:(k4 + 1) * 128], gtk[s][:, k4 * 128:(k4 + 1) * 128],
                                                IDb[:], start=True, stop=True),
                       reads=["gtk%d" % s, "IDb"], writes=["pb%d" % (2 + s)])
                op("act", lambda e: e.copy(gT[s][:], bank(2 + s)), reads=["pb%d" % (2 + s)], writes=["gT%d" % s])

            def op_mm(t):
                s = t % 2
                tcs = slice(t * 128, (t + 1) * 128)
                pm0 = 0 if s == 0 else 4
                for n in range(2):
                    for kc in range(8):
                        lhs = ATT[:, kc, tcs] if kc < 4 else gT[s][:, (kc - 4) * 128:(kc - 3) * 128]
                        op("pe", lambda e: e.matmul(bank(pm0 + n), lhs, WO[:, kc, n * 512:(n + 1) * 512],
                                                    start=(kc == 0), stop=(kc == 7)),
                           reads=["ATT", "gT%d" % s, "WO"], writes=["pbm%d" % s])
                post_residual(t, pm0, "pbm%d" % s, tmpo if s == 0 else tmpo2, "tmpw" if s == 0 else "tmpw2", GB, slot=s)

            op_tr(0)
            for t in range(ntile):
                if t + 1 < ntile:
                    op_tr(t + 1)
                op_mm(t)
            if dbg:
                dump("X1", X[:, 0, :], "X0")
            P.barrier()
            if stop == "oproj":
                break

            A.release(m1)
            prep_at(l, 1)
            GB = A.alloc("GB", [128, 2, D], F32)
            tmpo = A.alloc("tmpo", [128, D], F32)
            prep_gb(l, 1, GB, tmpo)
            WD = A.alloc("WD", [128, NF, D], BF16)
            fT = A.alloc("fT", [128, 8, 768], BF16)
            aT = A.alloc("aT", [128, NF, 768], BF16)
            wgu = [A.alloc("wgu%d" % i, [128, 8, 2, 128], BF16) for i in range(2)]
            sg = [A.alloc("sg%d" % i, [128, 384], F32) for i in range(2)]
            xsf = A.alloc("xsf", [128, D], F32)
            dma("pool", lambda e: [e.dma_start(out=WD[:, f_ * 11:(f_ + 1) * 11, :],
                                               in_=w_down[l, f_ * 1408:(f_ + 1) * 1408, :].rearrange("(f p) n -> p f n", p=128))
                                   for f_ in range(2)], writes=["WD"], n_dma=2)
            tl = list(range(ntile))
            it2 = 0
            grps = [tl[g0:g0 + 6] for g0 in range(0, ntile, 6)]
            for ti, t in enumerate(grps[0]):
                norm_transpose(t, fT, ti * 128, xsf, "xsf", "fT", 0)
            for gi, tiles in enumerate(grps):
                ntk = len(tiles) * 128
                chunks = [(c0, min(c0 + 384, ntk)) for c0 in range(0, ntk, 384)]
                for f in range(NF):
                    s = f % 2
                    dma("sp", lambda e: [e.dma_start(out=wgu[s][:].rearrange("p a b c -> p (a b c)"), in_=wsc_d[f])],
                        writes=["wgu%d" % s])
                    for (c0, c1) in chunks:
                        s2 = it2 % 2
                        it2 += 1
                        pg, pu = 2 + 2 * s2, 3 + 2 * s2
                        for kc in range(8):
                            op("pe", lambda e: e.matmul(bank(pg)[:, 0:c1 - c0], wgu[s][:, kc, 0, :], fT[:, kc, c0:c1],
                                                        start=(kc == 0), stop=(kc == 7)),
                               reads=["wgu%d" % s, "fT"], writes=["pb%d" % pg])
                        for kc in range(8):
                            op("pe", lambda e: e.matmul(bank(pu)[:, 0:c1 - c0], wgu[s][:, kc, 1, :], fT[:, kc, c0:c1],
                                                        start=(kc == 0), stop=(kc == 7)),
                               reads=["wgu%d" % s, "fT"], writes=["pb%d" % pu])
                        op("act", lambda e: e.activation(sg[s2][:, 0:c1 - c0], bank(pg)[:, 0:c1 - c0], AF.Silu),
                           reads=["pb%d" % pg], writes=["sg%d" % s2])
                        op("dve", lambda e: e.tensor_tensor(aT[:, f, c0:c1], sg[s2][:, 0:c1 - c0], bank(pu)[:, 0:c1 - c0], ALU.mult),
                           reads=["sg%d" % s2, "pb%d" % pu], writes=["aT"])
                nxt = grps[gi + 1] if gi + 1 < len(grps) else []
                for ti, t in enumerate(tiles):
                    pd0 = 6 if ti % 2 == 0 else 4
                    for n in range(2):
                        for f in range(NF):
                            op("pe", lambda e: e.matmul(bank(pd0 + n), aT[:, f, ti * 128:(ti + 1) * 128],
                                                        WD[:, f, n * 512:(n + 1) * 512], start=(f == 0), stop=(f == NF - 1)),
                               reads=["aT", "WD"], writes=["pb%d" % (pd0 + n)])
                    if ti < len(nxt):
                        norm_transpose(nxt[ti], fT, ti * 128, xsf, "xsf", "fT", 0)
                    post_residual(t, pd0, ["pb%d" % pd0, "pb%d" % (pd0 + 1)], tmpo, "tmpw", GB, slot=1)
                for ti in range(len(tiles), len(nxt)):
                    norm_transpose(nxt[ti], fT, ti * 128, xsf, "xsf", "fT", 0)
            if dbg:
                dump("X2", X[:, 0, :], "X0")
                dump("XC2", X[:, 16, :], "X16")

        for t in range(16):
            dma("sp", lambda e, t=t: [e.dma_start(out=out_d[t * 128:(t + 1) * 128, :], in_=X[:, t, :])],
                reads=["X%d" % t])
        P.finalize()
    return nc


_CONSTS = None


def kernel(**inputs):
    global _CONSTS
    if _CONSTS is None:
        _CONSTS = host_consts()
    nc = build()
    shared = {k: np.ascontiguousarray(np.asarray(v, dtype=np.float32)) for k, v in inputs.items()
              if k not in ("x", "c", "ctx")}
    shared.update(_CONSTS)
    x = np.asarray(inputs["x"], dtype=np.float32)
    c = np.asarray(inputs["c"], dtype=np.float32)
    ctx = np.asarray(inputs["ctx"], dtype=np.float32)
    in_maps = []
    for b in range(8):
        m = dict(shared)
        m["x"] = np.ascontiguousarray(x[b])
        m["c"] = np.ascontiguousarray(c[b])
        m["ctx"] = np.ascontiguousarray(ctx[b])
        in_maps.append(m)
    res = run_bass_kernel_spmd(nc, in_maps, core_ids=list(range(8)))
    return np.stack([r["out"] for r in res.results], axis=0).astype(np.float32)
```
